# Optimizing a Trainium2 kernel written in Bass

```python
import math
import jax, jax.numpy as jnp
from jax import lax
import numpy as np

D_MODEL = 2048
BATCH = 1
SEQ = 8192
DEPTH = 2

GRID_W = 64
HEAD_DIM = 128
ROPE_THETA = 500000.0
ROPE_DIM = HEAD_DIM // 4
NORM_EPS = 1e-6

NA_HEADS = 8
NA_KH_MAX = 8
NA_KW = 16

SW_Q_HEADS = 8
SW_KV_HEADS = 2
SW_WINDOW = 128
SW_BLOCK = 128

DIFF_HEADS = 8
Q_BLOCK = 128

MEM_LEN = 256
MEM_HEADS = 4

D_FF = 4 * D_MODEL

NA_W = NA_HEADS * HEAD_DIM
SW_QW = SW_Q_HEADS * HEAD_DIM
SW_KVW = SW_KV_HEADS * HEAD_DIM
EVEN_IN = 3 * NA_W + SW_QW + 2 * SW_KVW
EVEN_OUT = NA_W + SW_QW
DIFF_W = DIFF_HEADS * 2 * HEAD_DIM
ODD_IN = 3 * DIFF_W
MEM_W = MEM_HEADS * HEAD_DIM
N_EVEN = (DEPTH + 1) // 2
N_ODD = DEPTH // 2

kernel_name = "hybrid_natten_swa_diffattn_encoder"


def rms_norm(x, g):
    xf = x.astype(jnp.float32)
    y = xf * lax.rsqrt(jnp.mean(xf * xf, axis=-1, keepdims=True) + NORM_EPS)
    return (y * g.astype(jnp.float32)).astype(x.dtype)


def rope_tables(seq):
    inv = 1.0 / (ROPE_THETA ** (jnp.arange(0, ROPE_DIM, 2, dtype=jnp.float32) / ROPE_DIM))
    ang = jnp.arange(seq, dtype=jnp.float32)[:, None] * inv[None, :]
    return jnp.cos(ang), jnp.sin(ang)


def partial_rope(x, cos, sin):
    half = ROPE_DIM // 2
    x1 = x[..., :half].astype(jnp.float32)
    x2 = x[..., half:ROPE_DIM].astype(jnp.float32)
    c = cos[None, :, None, :]
    s = sin[None, :, None, :]
    r1 = (x1 * c - x2 * s).astype(x.dtype)
    r2 = (x2 * c + x1 * s).astype(x.dtype)
    return jnp.concatenate([r1, r2, x[..., ROPE_DIM:]], axis=-1)


def neighbourhood_attention(q, k, v, rpb):
    B, S, H, dh = q.shape
    rows = S // GRID_W
    kh = min(NA_KH_MAX, rows)
    kw = NA_KW
    qg = q.reshape(B, rows, GRID_W, H, dh)
    kg = k.reshape(B, rows, GRID_W, H, dh)
    vg = v.reshape(B, rows, GRID_W, H, dh)
    col = jnp.arange(GRID_W)
    col_start = jnp.clip(col - kw // 2, 0, GRID_W - kw)
    col_idx = col_start[:, None] + jnp.arange(kw)[None, :]
    col_off = col_idx - col[:, None] + (NA_KW - 1)
    scale = dh ** -0.5

    def one_row(i):
        rs = jnp.clip(i - kh // 2, 0, rows - kh)
        k_band = lax.dynamic_slice_in_dim(kg, rs, kh, axis=1)
        v_band = lax.dynamic_slice_in_dim(vg, rs, kh, axis=1)
        k_win = jnp.take(k_band, col_idx, axis=2)
        v_win = jnp.take(v_band, col_idx, axis=2)
        q_row = lax.dynamic_index_in_dim(qg, i, axis=1, keepdims=False)
        s = jnp.einsum('bjhd,brjchd->bhjrc', q_row, k_win).astype(jnp.float32) * scale
        row_off = rs + jnp.arange(kh) - i + (NA_KH_MAX - 1)
        bias = rpb[:, row_off[:, None, None], col_off[None, :, :]]
        s = s + jnp.transpose(bias, (0, 2, 1, 3))[None].astype(jnp.float32)
        p = jax.nn.softmax(s.reshape(B, H, GRID_W, kh * kw), axis=-1)
        p = p.reshape(B, H, GRID_W, kh, kw).astype(v.dtype)
        return jnp.einsum('bhjrc,brjchd->bjhd', p, v_win)

    out = lax.map(one_row, jnp.arange(rows))
    return jnp.transpose(out, (1, 0, 2, 3, 4)).reshape(B, S, H, dh)


def sliding_window_gqa(q, k, v, sinks, cos, sin):
    B, S, Hq, dh = q.shape
    Hkv = k.shape[2]
    G = Hq // Hkv
    nb = S // SW_BLOCK
    q = partial_rope(q, cos, sin)
    k = partial_rope(k, cos, sin)
    pad = ((0, 0), (SW_BLOCK, SW_BLOCK), (0, 0), (0, 0))
    kp = jnp.pad(k, pad).reshape(B, nb + 2, SW_BLOCK, Hkv, dh)
    vp = jnp.pad(v, pad).reshape(B, nb + 2, SW_BLOCK, Hkv, dh)
    k_band = jnp.concatenate([kp[:, :-2], kp[:, 1:-1], kp[:, 2:]], axis=2)
    v_band = jnp.concatenate([vp[:, :-2], vp[:, 1:-1], vp[:, 2:]], axis=2)
    qb = q.reshape(B, nb, SW_BLOCK, Hkv, G, dh)
    s = jnp.einsum('bnqkgd,bnckd->bnkgqc', qb, k_band).astype(jnp.float32) * (dh ** -0.5)
    blk = jnp.arange(nb)[:, None] * SW_BLOCK
    qpos = blk + jnp.arange(SW_BLOCK)[None, :]
    kpos = blk - SW_BLOCK + jnp.arange(3 * SW_BLOCK)[None, :]
    kp_b = kpos[:, None, :]
    valid = (jnp.abs(qpos[:, :, None] - kp_b) <= SW_WINDOW) & (kp_b >= 0) & (kp_b < S)
    s = jnp.where(valid[None, :, None, None, :, :], s, -jnp.inf)
    sink = sinks.astype(jnp.float32).reshape(Hkv, G)[None, None, :, :, None, None]
    m = jnp.maximum(jnp.max(s, axis=-1, keepdims=True), sink)
    p = jnp.exp(s - m)
    denom = jnp.sum(p, axis=-1, keepdims=True) + jnp.exp(sink - m)
    p = (p / denom).astype(v.dtype)
    o = jnp.einsum('bnkgqc,bnckd->bnqkgd', p, v_band)
    return o.reshape(B, S, Hq, dh)


def differential_attention(q, k, v, lam_q1, lam_k1, lam_q2, lam_k2, subln_g, lambda_init, cos, sin):
    B, S, H, _, dh = q.shape
    q = partial_rope(q.reshape(B, S, H * 2, dh), cos, sin).reshape(B, S, H, 2, dh)
    k = partial_rope(k.reshape(B, S, H * 2, dh), cos, sin).reshape(B, S, H, 2, dh)
    f32 = jnp.float32
    lam = (jnp.exp(jnp.sum(lam_q1.astype(f32) * lam_k1.astype(f32)))
           - jnp.exp(jnp.sum(lam_q2.astype(f32) * lam_k2.astype(f32))) + lambda_init)
    nb = S // Q_BLOCK
    qb = jnp.transpose(q.reshape(B, nb, Q_BLOCK, H, 2, dh), (1, 0, 2, 3, 4, 5))
    scale = dh ** -0.5

    def one_block(qblk):
        s = jnp.einsum('bqhtd,bkhtd->bhtqk', qblk, k).astype(f32) * scale
        p = jax.nn.softmax(s, axis=-1)
        a = p[:, :, 0] - lam * p[:, :, 1]
        return jnp.einsum('bhqk,bkhe->bqhe', a.astype(v.dtype), v)

    o = lax.map(one_block, qb)
    o = jnp.transpose(o, (1, 0, 2, 3, 4)).reshape(B, S, H, 2 * dh)
    o = rms_norm(o, subln_g) * (1.0 - lambda_init)
    return o.reshape(B, S, H * 2 * dh)


def memory_cross_attention(h, mem_n, wq, wk, wv, wo):
    B, S, _ = h.shape
    M = mem_n.shape[1]
    q = (h @ wq).reshape(B, S, MEM_HEADS, HEAD_DIM)
    k = (mem_n @ wk).reshape(B, M, MEM_HEADS, HEAD_DIM)
    v = (mem_n @ wv).reshape(B, M, MEM_HEADS, HEAD_DIM)
    s = jnp.einsum('bshd,bmhd->bhsm', q, k).astype(jnp.float32) * (HEAD_DIM ** -0.5)
    p = jax.nn.softmax(s, axis=-1).astype(v.dtype)
    o = jnp.einsum('bhsm,bmhd->bshd', p, v).reshape(B, S, MEM_W)
    return o @ wo


def squared_relu_mlp(h, w_up, w_down):
    u = jax.nn.relu(h @ w_up)
    return (u * u) @ w_down


def setup_inputs(seed: int = 0) -> dict:
    key = jax.random.key(seed)
    ks = jax.random.split(key, 32)
    f32 = jnp.float32

    def w(k, shape, fan_in):
        return jax.random.normal(k, shape, f32) * fan_in ** -0.5

    def gain(k, shape):
        return 1.0 + 0.05 * jax.random.normal(k, shape, f32)

    return {
        "x": jax.random.normal(ks[0], (BATCH, SEQ, D_MODEL), f32),
        "mem": jax.random.normal(ks[1], (BATCH, MEM_LEN, D_MODEL), f32),
        "even_w_in": w(ks[2], (N_EVEN, D_MODEL, EVEN_IN), D_MODEL),
        "even_w_out": w(ks[3], (N_EVEN, EVEN_OUT, D_MODEL), EVEN_OUT),
        "na_rpb": 0.5 * jax.random.normal(ks[4], (N_EVEN, NA_HEADS, 2 * NA_KH_MAX - 1, 2 * NA_KW - 1), f32),
        "sw_sinks": jax.random.normal(ks[5], (N_EVEN, SW_Q_HEADS), f32),
        "odd_w_in": w(ks[6], (N_ODD, D_MODEL, ODD_IN), D_MODEL),
        "odd_w_out": w(ks[7], (N_ODD, DIFF_W, D_MODEL), DIFF_W),
        "diff_lam_q1": 0.1 * jax.random.normal(ks[8], (N_ODD, HEAD_DIM), f32),
        "diff_lam_k1": 0.1 * jax.random.normal(ks[9], (N_ODD, HEAD_DIM), f32),
        "diff_lam_q2": 0.1 * jax.random.normal(ks[10], (N_ODD, HEAD_DIM), f32),
        "diff_lam_k2": 0.1 * jax.random.normal(ks[11], (N_ODD, HEAD_DIM), f32),
        "diff_subln_g": gain(ks[12], (N_ODD, 2 * HEAD_DIM)),
        "mix_pre_g": gain(ks[13], (DEPTH, D_MODEL)),
        "mix_post_g": gain(ks[14], (DEPTH, D_MODEL)),
        "mem_norm_g": gain(ks[15], (DEPTH, D_MODEL)),
        "mem_pre_g": gain(ks[16], (DEPTH, D_MODEL)),
        "mem_post_g": gain(ks[17], (DEPTH, D_MODEL)),
        "mem_wq": w(ks[18], (DEPTH, D_MODEL, MEM_W), D_MODEL),
        "mem_wk": w(ks[19], (DEPTH, D_MODEL, MEM_W), D_MODEL),
        "mem_wv": w(ks[20], (DEPTH, D_MODEL, MEM_W), D_MODEL),
        "mem_wo": w(ks[21], (DEPTH, MEM_W, D_MODEL), MEM_W),
        "mlp_pre_g": gain(ks[22], (DEPTH, D_MODEL)),
        "mlp_post_g": gain(ks[23], (DEPTH, D_MODEL)),
        "mlp_w_up": w(ks[24], (DEPTH, D_MODEL, D_FF), D_MODEL),
        "mlp_w_down": w(ks[25], (DEPTH, D_FF, D_MODEL), D_FF),
    }


def reference(x, mem, even_w_in, even_w_out, na_rpb, sw_sinks, odd_w_in, odd_w_out,
              diff_lam_q1, diff_lam_k1, diff_lam_q2, diff_lam_k2, diff_subln_g,
              mix_pre_g, mix_post_g, mem_norm_g, mem_pre_g, mem_post_g,
              mem_wq, mem_wk, mem_wv, mem_wo, mlp_pre_g, mlp_post_g, mlp_w_up, mlp_w_down):
    B, S, D = x.shape
    cos, sin = rope_tables(S)
    h = x
    for layer in range(DEPTH):
        hn = rms_norm(h, mix_pre_g[layer])
        if layer % 2 == 0:
            e = layer // 2
            proj = hn @ even_w_in[e]
            o0 = 0
            qa = proj[..., o0:o0 + NA_W]; o0 += NA_W
            ka = proj[..., o0:o0 + NA_W]; o0 += NA_W
            va = proj[..., o0:o0 + NA_W]; o0 += NA_W
            qs = proj[..., o0:o0 + SW_QW]; o0 += SW_QW
            kss = proj[..., o0:o0 + SW_KVW]; o0 += SW_KVW
            vs = proj[..., o0:o0 + SW_KVW]
            out_a = neighbourhood_attention(
                qa.reshape(B, S, NA_HEADS, HEAD_DIM), ka.reshape(B, S, NA_HEADS, HEAD_DIM),
                va.reshape(B, S, NA_HEADS, HEAD_DIM), na_rpb[e])
            out_b = sliding_window_gqa(
                qs.reshape(B, S, SW_Q_HEADS, HEAD_DIM), kss.reshape(B, S, SW_KV_HEADS, HEAD_DIM),
                vs.reshape(B, S, SW_KV_HEADS, HEAD_DIM), sw_sinks[e], cos, sin)
            y = jnp.concatenate([out_a.reshape(B, S, NA_W), out_b.reshape(B, S, SW_QW)], axis=-1)
            y = y @ even_w_out[e]
        else:
            o = layer // 2
            proj = hn @ odd_w_in[o]
            qd = proj[..., :DIFF_W].reshape(B, S, DIFF_HEADS, 2, HEAD_DIM)
            kd = proj[..., DIFF_W:2 * DIFF_W].reshape(B, S, DIFF_HEADS, 2, HEAD_DIM)
            vd = proj[..., 2 * DIFF_W:].reshape(B, S, DIFF_HEADS, 2 * HEAD_DIM)
            lambda_init = 0.8 - 0.6 * math.exp(-0.3 * layer)
            y = differential_attention(qd, kd, vd, diff_lam_q1[o], diff_lam_k1[o],
                                       diff_lam_q2[o], diff_lam_k2[o], diff_subln_g[o],
                                       lambda_init, cos, sin)
            y = y @ odd_w_out[o]
        h = h + rms_norm(y, mix_post_g[layer])
        mem_n = rms_norm(mem, mem_norm_g[layer])
        c = memory_cross_attention(rms_norm(h, mem_pre_g[layer]), mem_n, mem_wq[layer],
                                   mem_wk[layer], mem_wv[layer], mem_wo[layer])
        h = h + rms_norm(c, mem_post_g[layer])
        f = squared_relu_mlp(rms_norm(h, mlp_pre_g[layer]), mlp_w_up[layer], mlp_w_down[layer])
        h = h + rms_norm(f, mlp_post_g[layer])
    return h
```

```python
import math
import numpy as np
import concourse.bass as bass
import concourse.mybir as mybir
from concourse.bass_utils import run_bass_kernel_spmd

F32 = mybir.dt.float32
BF16 = mybir.dt.bfloat16
AF = mybir.ActivationFunctionType
ALU = mybir.AluOpType
AX = mybir.AxisListType

NCORES = 8
S = 8192
D = 2048
DC = 16
T = 1024
NEG = -30000.0
EPS = 1e-6
SCALE = 128 ** -0.5

ENGS = ("pe", "act", "dve", "pool", "sp")


def _prod(xs):
    r = 1
    for x in xs:
        r *= x
    return r


class Alloc:
    def __init__(self, space, name, lo, ntiles, tile_bytes, ap):
        self.space, self.name, self.lo = space, name, lo
        self.ntiles, self.tb = ntiles, tile_bytes
        self.hi = lo + ntiles * tile_bytes
        self.ap = ap
        self.ovl = []
        self.w = {}
        self.r = {}

    def __getitem__(self, i):
        return self.ap[:, i]

    def t(self, i=None, j=None):
        if i is None:
            return (self, 0, self.ntiles)
        return (self, i, (i + 1) if j is None else j)


class Sched:
    def __init__(self):
        self.ops = {e: [] for e in ENGS}
        self.seen = {e: {} for e in ENGS}
        self.allocs = []
        self.dma_cnt = {}
        self.qsems = {}
        self.qnext = {}
        self.dry = False

    def register(self, a):
        if a.space in ("sb", "ps"):
            for b in self.allocs:
                if b.space == a.space and b.lo < a.hi and a.lo < b.hi:
                    a.ovl.append(b)
                    b.ovl.append(a)
        self.allocs.append(a)
        return a

    def _tiles(self, ref):
        a, i0, i1 = ref
        for i in range(i0, i1):
            yield a, i
        if a.ovl:
            lo = a.lo + i0 * a.tb
            hi = a.lo + i1 * a.tb
            for b in a.ovl:
                j0 = max(0, (lo - b.lo) // b.tb)
                j1 = min(b.ntiles, -((b.lo - hi) // b.tb))
                for j in range(j0, j1):
                    yield b, j

    def op(self, eng, fn, reads=(), writes=(), dma=None):
        if self.dry:
            return None
        deps = {}

        def add(tok, raw):
            src, val = tok
            if src == eng and not dma and (eng == "pe" or not raw):
                return
            if val > deps.get(src, 0):
                deps[src] = val

        for ref in reads:
            for a, i in self._tiles(ref):
                w = a.w.get(i)
                if w is not None:
                    add(w, True)
        for ref in writes:
            for a, i in self._tiles(ref):
                w = a.w.get(i)
                if w is not None:
                    add(w, False)
                rr = a.r.get(i)
                if rr:
                    for src, val in rr.items():
                        add((src, val), False)
        ops = self.ops[eng]
        idx = len(ops) + 1
        waits = []
        seen = self.seen[eng]
        if dma:
            names = self.qsems[dma]
            k = self.qnext[dma]
            self.qnext[dma] = (k + 1) % len(names)
            sem = names[k]
            inc = 1 if dma == "cc" else 16
            prev = self.dma_cnt.get(sem, 0)
            if prev:
                add((sem, prev), True)
            self.dma_cnt[sem] = prev + inc
            tok = (sem, prev + inc)
        else:
            sem = None
            tok = (eng, idx)
        for src, val in deps.items():
            if seen.get(src, 0) >= val:
                continue
            seen[src] = val
            waits.append((src, val))
            if src in self.ops:
                self.ops[src][val - 1]["sig"] = True
        ops.append({"fn": fn, "waits": waits, "sig": False, "dsem": sem, "dinc": 1 if dma == "cc" else 16})
        for ref in reads:
            a, i0, i1 = ref
            for i in range(i0, i1):
                a.r.setdefault(i, {})[tok[0]] = tok[1]
        for ref in writes:
            a, i0, i1 = ref
            for i in range(i0, i1):
                a.w[i] = tok
                a.r[i] = {}
        return tok

    def all_tokens_wait(self, eng, toks):
        ops = self.ops[eng]
        waits = []
        for src, val in toks:
            waits.append((src, val))
            if src in self.ops:
                self.ops[src][val - 1]["sig"] = True
        ops.append({"fn": None, "waits": waits, "sig": False, "dsem": None})

    def emit(self, nc, sems, block):
        cum = {}
        for e in ENGS:
            c = 0
            lst = []
            for o in self.ops[e]:
                if o["sig"]:
                    c += 1
                lst.append(c)
            cum[e] = lst
        handles = {"pe": block.tensor, "act": block.scalar, "dve": block.vector,
                   "pool": block.gpsimd, "sp": block.sync}

        def make(e):
            def body(eng):
                for o in self.ops[e]:
                    for src, val in o["waits"]:
                        if src in self.ops:
                            eng.wait_ge(sems[src], cum[src][val - 1])
                        else:
                            eng.wait_ge(sems[src], val)
                    if o["fn"] is None:
                        continue
                    ins = o["fn"](eng)
                    if o["dsem"] is not None:
                        ins.then_inc(sems[o["dsem"]], o["dinc"])
                    if o["sig"]:
                        ins.then_inc(sems[e], 1)
            return body

        for e in ENGS:
            if self.ops[e]:
                handles[e](make(e))


WCOLS = 256
WLOOK = 12
WMATS = [("even_w_in", 0, 2048, 4608), ("even_w_out", 0, 2048, 2048),
         ("mem_wk", 0, 2048, 512), ("mem_wv", 0, 2048, 512), ("mem_wq", 0, 2048, 512), ("mem_wo", 0, 512, 2048),
         ("mlp_w_up", 0, 2048, 8192), ("mlp_w_down", 0, 8192, 2048),
         ("odd_w_in", 0, 2048, 6144), ("odd_w_out", 0, 2048, 2048),
         ("mem_wk", 1, 2048, 512), ("mem_wv", 1, 2048, 512), ("mem_wq", 1, 2048, 512), ("mem_wo", 1, 512, 2048),
         ("mlp_w_up", 1, 2048, 8192), ("mlp_w_down", 1, 8192, 2048)]
WOFF = {}
WIDX = {}
_o = 0
for _i, (_n, _l, _K, _N) in enumerate(WMATS):
    WOFF[(_n, _l)] = (_o, _K, _N)
    WIDX[(_n, _l)] = _i
    _o += _K * _N // 8
WTOT8 = _o
CAT_OFF = 108 * 1024
NW = 3
GI = {"mix_pre": 0, "mix_post": 2, "mem_norm": 4, "mem_pre": 6, "mem_post": 8, "mlp_pre": 10, "mlp_post": 12}


class Ctx:
    SB_BYTES = 190 * 1024

    def __init__(self, nc, sb, ps, dr, wplan=None, dry=False):
        self.nc = nc
        self.s = Sched()
        self.s.dry = dry
        self.sb = sb
        self.ps = ps
        self.dr = dr
        self.top = 0
        self.s.qsems = {"sp": ["dsp%d" % i for i in range(12)], "pool": ["dpl%d" % i for i in range(6)],
                        "cc": ["dcc%d" % i for i in range(4)]}
        self.s.qnext = {"sp": 0, "pool": 0, "cc": 0}
        self.wrecd = set()
        self.banks = self.s.register(Alloc("ps", "psum", 0, 8, 2048,
                                           ps.rearrange("p (b n) -> p b n", b=8)))
        self.da = {}
        self.g = {}
        self.wplan = wplan
        self.wrec = []
        self.wi = 0
        self.wissued = 0
        self.rr = 0
        self.dumps = []

    def sem_names(self):
        n = list(ENGS)
        for q in self.s.qsems.values():
            n += q
        return n

    def alloc(self, name, ntiles, tile_shape, dt, at=None):
        esz = 4 if dt == F32 else 2
        n = _prod(tile_shape)
        tb = n * esz
        lo = self.top if at is None else at
        lo = (lo + 31) // 32 * 32
        hi = lo + ntiles * tb
        assert hi <= self.SB_BYTES, ("SBUF overflow", name, hi)
        if at is None:
            self.top = hi
        ap = self.sb[:, lo // 4:(hi + 3) // 4]
        if dt != F32:
            ap = ap.bitcast(dt)
        flat = ap[:, 0:ntiles * n]
        names = " ".join("d%d" % i for i in range(len(tile_shape)))
        kw = {"d%d" % i: v for i, v in enumerate(tile_shape)}
        ap = flat.rearrange("p (t %s) -> p t %s" % (names, names), t=ntiles, **kw)
        a = self.s.register(Alloc("sb", name, lo, ntiles, tb, ap))
        a.flat = flat
        return a

    def dten(self, name, ntiles=1):
        if name not in self.da:
            self.da[name] = self.s.register(Alloc("dram:" + name, name, 0, ntiles, 1, None))
        return self.da[name]

    def bank(self, i, j=None):
        return self.banks.t(i, j)

    def pb(self, i):
        return self.banks[i]

    def pb2(self, i, n):
        return self.ps[:, i * 512:i * 512 + n]

    def mm(self, out, lhsT, rhs, start, stop, reads, writes):
        return self.s.op("pe", lambda e: e.matmul(out, lhsT, rhs, start=start, stop=stop), reads, writes)

    def tr(self, out, in_, ident, reads, writes):
        return self.s.op("pe", lambda e: e.transpose(out, in_, ident), reads, writes)

    def act(self, out, in_, func, reads, writes, bias=None, scale=None, accum_out=None):
        kw = {}
        if bias is not None:
            kw["bias"] = bias
        if scale is not None:
            kw["scale"] = scale
        if accum_out is not None:
            kw["accum_out"] = accum_out
        return self.s.op("act", lambda e: e.activation(out=out, in_=in_, func=func, **kw), reads, writes)

    def copy(self, out, in_, reads, writes, eng=None):
        self.rr += 1
        if (self.rr % 2 and eng is None) or eng == "act":
            return self.act(out, in_, AF.Copy, reads, writes)
        return self.s.op("dve", lambda e: e.tensor_copy(out=out, in_=in_), reads, writes)

    def dma(self, q, out, in_, reads, writes):
        return self.s.op(q, lambda e: e.dma_start(out=out, in_=in_), reads, writes, dma=q)

    def dv(self, meth, kw, reads, writes, eng="dve"):
        return self.s.op(eng, lambda e: getattr(e, meth)(**kw), reads, writes)

    def pl(self, meth, kw, reads, writes):
        return self.dv(meth, kw, reads, writes, eng="pool")

    def wget(self, spec):
        if self.s.dry:
            self.wrec.append(spec)
            return self.g["wslots"][0]
        i = self.wi
        self.wi += 1
        assert self.wplan[i] == spec, (i, self.wplan[i], spec)
        while self.wissued < min(len(self.wplan), i + NW):
            self._wissue(self.wissued)
            self.wissued += 1
        return self.g["wslots"][i % NW]

    def _ensure_mat(self, key, phase=None, after=()):
        off, K, N = WOFF[key]
        sz = K * N // NCORES
        wl = self.dten("wl", len(WMATS))
        wf = self.dten("wf_%s%d" % key, 1)
        mi = WIDX[key]
        if phase in (None, "cast") and (key, "cast") not in self.wrecd:
            self.wrecd.add((key, "cast"))
            self.dma("pool", self.dr["wl"][off:off + sz].rearrange("(a b) -> a b", a=64),
                     self.dr["wshard"][off:off + sz].rearrange("(a b) -> a b", a=64), list(after), [wl.t(mi)])
        if phase in (None, "coll") and (key, "coll") not in self.wrecd:
            self.wrecd.add((key, "coll"))
            i_ap = self.dr["wl"][off:off + sz].rearrange("(k n) -> k n", n=min(K, 2048) * 2)
            o_ap = self.dr["wf_%s%d" % key]
            grp = [list(range(NCORES))]
            self.s.op("pool", lambda e: e.collective_compute("AllGather", ALU.bypass, replica_groups=grp,
                                                             ins=[i_ap.opt()], outs=[o_ap.opt()]),
                      [wl.t(mi)], [wf.t()], dma="cc")

    def ensure_mats(self, keys, after=()):
        if self.s.dry:
            return
        used = set((p[0], p[1]) for p in self.wplan)
        for key in keys:
            if key in used:
                self._ensure_mat(key, after=after)

    def _wissue(self, i):
        for j in range(i, min(len(self.wplan), i + WLOOK)):
            self._ensure_mat((self.wplan[j][0], self.wplan[j][1]))
        name, layer, k0, nk, c0, ncols = self.wplan[i]
        slot = self.g["wslots"][i % NW]
        wf = self.dten("wf_%s%d" % (name, layer), 1)
        N = WOFF[(name, layer)][2]
        j = (k0 // (nk * 128)) * (N // WCOLS) + c0 // WCOLS
        src = self.dr["wf_%s%d" % (name, layer)][j * 128:(j + 1) * 128, :]
        self.dma("sp", slot.flat[:, 0:nk * WCOLS], src, [wf.t()], [slot.t()])

    def dump(self, name, ap, shape, dt, reads):
        if self.s.dry:
            return
        t = self.nc.dram_tensor("dbg_" + name, list(shape), dt, kind="ExternalOutput").ap()
        self.dumps.append("dbg_" + name)
        tok = self.dma("sp", t, ap, reads, [self.dten("dbg_" + name).t()])
        self.g.setdefault("final", []).append(tok)


def gain(c, key, layer, ch):
    gi = GI[key] + layer
    return c.g["gains"][0][:, gi * 16 + ch:gi * 16 + ch + 1]


def load_consts(c):
    g = c.g
    g["ident"] = c.alloc("ident", 1, [128], F32)
    c.dma("sp", g["ident"][0], c.dr["ident"], [], [g["ident"].t()])
    g["ones"] = c.alloc("ones", 1, [128], BF16)
    c.dv('memset', dict(ap=g["ones"][0], constant=1.0), [], [g["ones"].t()])
    g["perm"] = c.alloc("perm", 1, [32], BF16)
    c.dma("pool", g["perm"][0][0:32, :], c.dr["perm"], [], [g["perm"].t()])
    g["gains"] = c.alloc("gains", 1, [14 * 16], F32)
    gt = g["gains"]
    c.dma("sp", gt[0], c.dr["gains"], [], [gt.t()])
    c.dv('tensor_scalar', dict(out=gt[0], in0=gt[0], scalar1=float(math.sqrt(D)), scalar2=None, op0=ALU.mult),
          [gt.t()], [gt.t()])
    g["sq"] = c.alloc("sq", 3, [512], BF16)
    g["rstd"] = c.alloc("rstd", 2, [512], F32)
    g["wslots"] = [c.alloc("w%d" % i, 1, [16, WCOLS], BF16) for i in range(NW)]
    g["sqi"] = 0
    g["rsi"] = 0
    g["ssb"] = 0
    g["base"] = c.top


def rms_stats(c, xap, xres, ntok, n_ch=16):
    g = c.g
    bk = 4 + (g["ssb"] % 2)
    g["ssb"] += 1
    for ch in range(n_ch):
        k = g["sqi"] % 3
        g["sqi"] += 1
        sq = g["sq"]
        c.act(sq[k][:, 0:ntok], xap(ch), AF.Square, [xres(ch)], [sq.t(k)])
        c.mm(c.pb(bk)[:, 0:ntok], g["ones"][0], sq[k][:, 0:ntok], ch == 0, ch == n_ch - 1,
             [sq.t(k), g["ones"].t()], [c.bank(bk)])
    r = g["rsi"] % 2
    g["rsi"] += 1
    rs = g["rstd"]
    c.act(rs[r][:, 0:ntok], c.pb(bk)[:, 0:ntok], AF.Sqrt, [c.bank(bk)], [rs.t(r)], bias=float(D * EPS), scale=1.0)
    c.dv('reciprocal', dict(out=rs[r][:, 0:ntok], in_=rs[r][:, 0:ntok]), [rs.t(r)], [rs.t(r)])
    return rs[r][:, 0:ntok], rs.t(r)


def rms_apply(c, xap, xres, key, layer, rstd, rres, outap, outres, n_ch=16):
    for ch in range(n_ch):
        gp = gain(c, key, layer, ch)
        c.dv('scalar_tensor_tensor', dict(out=outap(ch), in0=xap(ch), scalar=gp, in1=rstd,
                                                              op0=ALU.mult, op1=ALU.mult),
              [xres(ch), rres, c.g["gains"].t()], [outres(ch)])


def transpose_in(c, src_rows, ntile_rows, dst_ap, dst_res, xtok):
    g = c.g
    for e in range(ntile_rows):
        xb = e % 2
        c.dma("sp", xtok[xb], src_rows(e), [], [xtok.t(xb)])
        for g4 in range(4):
            bk = g4
            for k in range(4):
                ch = g4 * 4 + k
                c.tr(c.pb(bk)[:, k * 128:(k + 1) * 128], xtok[xb][:, ch * 128:(ch + 1) * 128], g["ident"][0],
                     [xtok.t(xb), g["ident"].t()], [c.bank(bk)])
            c.copy(dst_ap(g4, e), c.pb(bk).rearrange("p (k n) -> p k n", k=4), [c.bank(bk)], dst_res(g4, e))


def proj_fm(c, spec, nk, rhs, nblk, evac, bank0=0):
    w = c.wget(spec)
    wa = w.ap[:, 0]
    nj = spec[5] // 128
    for j in range(nj):
        base = bank0 + (c.g.setdefault("pjrot", 0) % 2) * nblk
        c.g["pjrot"] += 1
        for kc in range(nk):
            for b in range(nblk):
                r_ap, r_res = rhs(kc, b)
                n = r_ap.shape[-1]
                c.mm(c.pb(base + b)[:, 0:n], wa[:, kc, j * 128:(j + 1) * 128], r_ap, kc == 0, kc == nk - 1,
                     [w.t()] + r_res, [c.bank(base + b)])
        for b in range(nblk):
            evac(j, b, base + b)


def proj_tm(c, spec, nk, lhs, evac, bank0=6):
    w = c.wget(spec) if not isinstance(spec, Alloc) else spec
    wa = w.ap[:, 0]
    bk = bank0 + (c.g.setdefault("ptrot", 0) % 2)
    c.g["ptrot"] += 1
    ncols = WCOLS
    for kc in range(nk):
        l_ap, l_res = lhs(kc)
        c.mm(c.pb(bk)[:, 0:ncols], l_ap, wa[:, kc, 0:ncols], kc == 0, kc == nk - 1, [w.t()] + l_res, [c.bank(bk)])
    evac(bk)


def rope_fm(c, dst, dres, ntok, tab0):
    g = c.g
    rp = g["rope"]
    bk = 6 + (g.setdefault("rprot", 0) % 2)
    g["rprot"] += 1
    k = g["rprot"] % 2
    tmp = g["ropetmp"]
    c.mm(c.pb(bk)[0:32, 0:ntok], g["perm"][0][0:32, :], dst[0:32, :], True, True, [g["perm"].t()] + dres, [c.bank(bk)])
    c.dv('tensor_tensor', dict(out=tmp[2 * k][0:32, 0:ntok], in0=c.pb(bk)[0:32, 0:ntok],
                                    in1=rp[1][0:32, tab0:tab0 + ntok], op=ALU.mult),
          [c.bank(bk), rp.t(1)], [tmp.t(2 * k)])
    c.dv('tensor_tensor', dict(out=tmp[2 * k + 1][0:32, 0:ntok], in0=dst[0:32, :],
                                    in1=rp[0][0:32, tab0:tab0 + ntok], op=ALU.mult),
          dres + [rp.t(0)], [tmp.t(2 * k + 1)])
    c.dv('tensor_tensor', dict(out=dst[0:32, :], in0=tmp[2 * k][0:32, 0:ntok], in1=tmp[2 * k + 1][0:32, 0:ntok],
                                    op=ALU.add), [tmp.t(2 * k), tmp.t(2 * k + 1)], dres)


def l0_front(c):
    g = c.g
    c.top = g["base"]
    hn = c.alloc("hnT", 16 * 12, [128], BF16)
    g["hnT"] = hn
    mark = c.top
    xtok = c.alloc("xtok", 2, [2048], F32)
    xT = c.alloc("xT", 2 * 16, [512], F32)
    hT = c.dten("hT", 2)
    x_ext = c.dr["x_ext"]
    for b in range(3):
        buf = b % 2
        transpose_in(c, lambda e4: x_ext[(b * 4 + e4) * 128:(b * 4 + e4 + 1) * 128, :], 4,
                     lambda g4, e4: xT.ap[:, buf * 16 + g4 * 4:buf * 16 + g4 * 4 + 4, e4 * 128:(e4 + 1) * 128],
                     lambda g4, e4: [xT.t(buf * 16 + g4 * 4, buf * 16 + g4 * 4 + 4)], xtok)
        lo, hi = max(256, b * 512), min(1280, b * 512 + 512)
        c.dma("sp", c.dr["hT"][:, :, lo - 256:hi - 256], xT.ap[:, buf * 16:buf * 16 + 16, lo - b * 512:hi - b * 512],
              [xT.t(buf * 16, buf * 16 + 16)], [hT.t()])
        rstd, rres = rms_stats(c, lambda ch: xT[buf * 16 + ch], lambda ch: xT.t(buf * 16 + ch), 512)
        hflat = hn.flat.rearrange("p (c t) -> p c t", c=16)
        rms_apply(c, lambda ch: xT[buf * 16 + ch], lambda ch: xT.t(buf * 16 + ch), "mix_pre", 0, rstd, rres,
                  lambda ch: hflat[:, ch, b * 512:(b + 1) * 512], lambda ch: hn.t(ch * 12 + b * 4, ch * 12 + b * 4 + 4))
    c.top = mark


def hn_rhs(hn, ntile_per_chunk, t0, n):
    fl = hn.flat.rearrange("p (c t) -> p c t", c=16)

    def f(kc):
        return fl[:, kc, t0:t0 + n], [hn.t(kc * ntile_per_chunk + t0 // 128, kc * ntile_per_chunk + (t0 + n + 127) // 128)]
    return f


def l0_inproj(c):
    g = c.g
    hn = g["hnT"]
    rp = c.alloc("rope", 2, [1280], F32)
    g["rope"] = rp
    c.dma("sp", rp.ap[0:32], c.dr["rope"], [], [rp.t()])
    g["ropetmp"] = c.alloc("ropetmp", 4, [512], F32)
    vA = c.alloc("vA", 12, [8, 130], BF16)
    vS = c.alloc("vS", 10, [2, 130], BF16)
    g["vA"], g["vS"] = vA, vS
    c.dv('memset', dict(ap=vA.ap[:, :, :, 128:130], constant=1.0), [], [vA.t()])
    c.dv('memset', dict(ap=vS.ap[:, :, :, 128:130], constant=1.0), [], [vS.t()])
    mark = c.top
    stg = c.alloc("stg", 8, [512], BF16)
    si = [0]

    def fm_group(wname, col0, nheads, dname, tok_blocks, rope_tab):
        dt_ = c.dten(dname, nheads)
        for t in range(nheads * 128 // WCOLS):
            spec = (wname, 0, 0, 16, col0 + t * WCOLS, WCOLS)

            def rhs(kc, b):
                t0, n = tok_blocks[b]
                return hn_rhs(hn, 12, t0, n)(kc)

            def evac(j, b, bk):
                h = t * (WCOLS // 128) + j
                t0, n = tok_blocks[b]
                k = si[0] % 8
                si[0] += 1
                c.copy(stg[k][:, 0:n], c.pb(bk)[:, 0:n], [c.bank(bk)], [stg.t(k)])
                if rope_tab is not None:
                    rope_fm(c, stg[k][:, 0:n], [stg.t(k)], n, rope_tab + t0 - tok_blocks[0][0])
                c.dma("sp", c.dr[dname][h, :, t0 - tok_blocks[0][0]:t0 - tok_blocks[0][0] + n], stg[k][:, 0:n],
                      [stg.t(k)], [dt_.t(h)])
            proj_fm(c, spec, 16, rhs, len(tok_blocks), evac)

    own = [(256, 512), (768, 512)]
    fm_group("even_w_in", 0, 8, "qaT", own, None)
    fm_group("even_w_in", 1024, 8, "kaT", [(0, 512), (512, 512), (1024, 512)], None)
    for t in range(1024 // WCOLS):
        spec = ("even_w_in", 0, 0, 16, 2048 + t * WCOLS, WCOLS)
        w = c.wget(spec)
        for e in range(12):
            def evac(bk, e=e, t=t):
                nh = WCOLS // 128
                c.copy(vA.ap[:, e, t * nh:(t + 1) * nh, 0:128], c.pb(bk)[:, 0:WCOLS].rearrange("p (h d) -> p h d", h=nh),
                       [c.bank(bk)], [vA.t(e)])
            proj_tm(c, w, 16, lambda kc, e=e: hn_rhs(hn, 12, e * 128, 128)(kc), evac)
    fm_group("even_w_in", 3072, 8, "qsT", own, 128)
    fm_group("even_w_in", 4096, 2, "ksT", [(128, 512), (640, 512), (1152, 256)], 0)
    spec = ("even_w_in", 0, 0, 16, 4352, WCOLS)
    w = c.wget(spec)
    for e in range(10):
        def evac(bk, e=e):
            c.copy(vS.ap[:, e, :, 0:128], c.pb(bk)[:, 0:256].rearrange("p (h d) -> p h d", h=2), [c.bank(bk)], [vS.t(e)])
        proj_tm(c, w, 16, lambda kc, e=e: hn_rhs(hn, 12, (e + 1) * 128, 128)(kc), evac)
    c.top = mark


def pipeline(n, stages):
    ns = len(stages)
    for tick in range(n + ns - 1):
        for s, f in enumerate(stages):
            j = tick - s
            if 0 <= j < n:
                f(j)


def tail_norm(c, acc_bk, ncol, extra_den, den, on, k):
    g = c.g
    a = c.pb(acc_bk)
    if extra_den is not None:
        c.dv('tensor_tensor', dict(out=den[k][:, 0:1], in0=a[:, ncol:ncol + 1], in1=extra_den, op=ALU.add),
             [c.bank(acc_bk), g["esink"].t()], [den.t(k)])
        c.dv('reciprocal', dict(out=den[k][:, 0:1], in_=den[k][:, 0:1]), [den.t(k)], [den.t(k)])
    else:
        c.dv('reciprocal', dict(out=den[k][:, 0:1], in_=a[:, ncol:ncol + 1]), [c.bank(acc_bk)], [den.t(k)])
    c.dv('tensor_scalar', dict(out=on[k], in0=a[:, 0:ncol], scalar1=den[k][:, 0:1], scalar2=None, op0=ALU.mult),
         [c.bank(acc_bk), den.t(k)], [on.t(k)])


def tail_tr(c, on, k, dst, dres, tb):
    g = c.g
    c.tr(c.pb(tb)[:, 0:128], on[k], g["ident"][0], [on.t(k), g["ident"].t()], [c.bank(tb)])
    c.copy(dst, c.pb(tb)[:, 0:128], [c.bank(tb)], dres)


def l0_attn(c):
    g = c.g
    vA, vS = g["vA"], g["vS"]
    c.ensure_mats([(m[0], m[1]) for m in WMATS[6:8]], after=[c.dten("ksT", 2).t()])
    cat = c.alloc("catT", 16 * 2, [512], BF16, at=g["base"] + CAT_OFF)
    g["catT"] = cat
    catf = cat.flat.rearrange("p (c t) -> p c t", c=16)
    mark = c.top
    c.top = g["base"]
    qb = c.alloc("qb", 2, [1024], BF16)
    kb = c.alloc("kb", 2, [1536], BF16)
    kbs = c.alloc("kbs", 2, [1280], BF16)
    bias = c.alloc("nabias", 3, [768], F32)
    sc = c.alloc("sc", 3, [768], F32)
    pt = c.alloc("pt", 3, [768], BF16)
    den = c.alloc("den", 3, [8], F32)
    on = c.alloc("on", 3, [128], F32)
    swm = c.alloc("swm", 8, [384], F32)
    esink = c.alloc("esink", 1, [8], F32)
    g["esink"] = esink
    c.dma("sp", swm.ap, c.dr["sw_mask"], [], [swm.t()])
    c.dma("sp", esink[0], c.dr["sinks"], [], [esink.t()])
    c.act(esink[0], esink[0], AF.Exp, [esink.t()], [esink.t()])
    qda, kda = c.dten("qaT", 8), c.dten("kaT", 8)
    qds, kds = c.dten("qsT", 8), c.dten("ksT", 2)
    for gq in range(2):
        c.dma("sp", kbs[gq], c.dr["ksT"][gq], [kds.t(gq)], [kbs.t(gq)])
    items = [("na", h, a) for h in range(8) for a in range(8)] + [("sw", h, a) for h in range(8) for a in range(8)]

    def stA(i):
        kind, h, a = items[i]
        k, k2 = i % 3, i % 2
        hb = (i // 8) % 2
        sb0 = 2 * k2
        if kind == "na":
            if a == 0:
                c.dma("sp", qb[hb], c.dr["qaT"][h], [qda.t(h)], [qb.t(hb)])
                c.dma("sp", kb[hb], c.dr["kaT"][h], [kda.t(h)], [kb.t(hb)])
            e0 = a if a < 6 else a - 1
            c.dma("sp", bias[k], c.dr["na_bias"][a, h], [], [bias.t(k)])
            for j in range(6):
                c.mm(c.pb2(sb0, 1024)[:, j * 128:(j + 1) * 128], kb[hb][:, (e0 + j) * 128:(e0 + j + 1) * 128],
                     qb[hb][:, a * 128:(a + 1) * 128], True, True, [kb.t(hb), qb.t(hb)], [c.bank(sb0 + j // 4)])
            c.dv('scalar_tensor_tensor', dict(out=sc[k], in0=c.pb2(sb0, 768), scalar=float(SCALE), in1=bias[k],
                                              op0=ALU.mult, op1=ALU.add), [c.bank(sb0, sb0 + 2), bias.t(k)], [sc.t(k)])
            c.act(pt[k], sc[k], AF.Exp, [sc.t(k)], [pt.t(k)])
        else:
            gq = h // 4
            if a == 0:
                c.dma("sp", qb[hb], c.dr["qsT"][h], [qds.t(h)], [qb.t(hb)])
            for j in range(3):
                c.mm(c.pb(sb0)[:, j * 128:(j + 1) * 128], kbs[gq][:, (a + j) * 128:(a + j + 1) * 128],
                     qb[hb][:, a * 128:(a + 1) * 128], True, True, [kbs.t(gq), qb.t(hb)], [c.bank(sb0)])
            c.dv('scalar_tensor_tensor', dict(out=sc[k][:, 0:384], in0=c.pb(sb0)[:, 0:384], scalar=float(SCALE),
                                              in1=swm[a], op0=ALU.mult, op1=ALU.add), [c.bank(sb0), swm.t(a)], [sc.t(k)])
            c.act(pt[k][:, 0:384], sc[k][:, 0:384], AF.Exp, [sc.t(k)], [pt.t(k)])

    def stB(i):
        kind, h, a = items[i]
        k, ab = i % 3, 4 + i % 2
        if kind == "na":
            e0 = a if a < 6 else a - 1
            for j in range(6):
                c.mm(c.pb(ab)[:, 0:129], pt[k][:, j * 128:(j + 1) * 128], vA.ap[:, e0 + j, h, 0:129], j == 0, j == 5,
                     [pt.t(k), vA.t(e0 + j)], [c.bank(ab)])
            tail_norm(c, ab, 128, None, den, on, k)
        else:
            gq = h // 4
            for j in range(3):
                c.mm(c.pb(ab)[:, 0:129], pt[k][:, j * 128:(j + 1) * 128], vS.ap[:, a + j, gq, 0:129], j == 0, j == 2,
                     [pt.t(k), vS.t(a + j)], [c.bank(ab)])
            tail_norm(c, ab, 128, esink[0][:, h:h + 1], den, on, k)

    def stC(i):
        kind, h, a = items[i]
        ch = h if kind == "na" else 8 + h
        tail_tr(c, on, i % 3, catf[:, ch, a * 128:(a + 1) * 128], [cat.t(ch * 2 + a // 4)], 6 + i % 2)

    pipeline(len(items), [stA, stB, stC])
    c.top = mark


def proj_Y(c, wname, layer, nk, src, Y):
    yf = Y.flat.rearrange("p (c t) -> p c t", c=16)
    for t in range(2048 // WCOLS):
        spec = (wname, layer, 0, nk, t * WCOLS, WCOLS)

        def evac(j, b, bk):
            n = t * (WCOLS // 128) + j
            c.copy(yf[:, n, b * 512:(b + 1) * 512], c.pb(bk), [c.bank(bk)], [Y.t(n * 2 + b)])
        proj_fm(c, spec, nk, src, 2, evac)


def post_pre(c, Y, layer, post_key, pre_key, pre_layer, hn_out, final=False):
    g = c.g
    mark = c.top
    H = c.alloc("H", 16, [512], F32)
    hT = c.dten("hT", 2)
    tmp = c.alloc("pp_tmp", 4, [512], F32)
    if final:
        otok = c.alloc("otok", 2, [2048], F32)
    for b in range(2):
        rstd, rres = rms_stats(c, lambda ch: Y[ch * 2 + b], lambda ch: Y.t(ch * 2 + b), 512)
        c.dma("sp", H.ap, c.dr["hT"][:, :, b * 512:(b + 1) * 512], [hT.t(b)], [H.t()])
        for ch in range(16):
            k = ch % 4
            gp = gain(c, post_key, layer, ch)
            c.pl('tensor_tensor', dict(out=tmp[k], in0=Y[ch * 2 + b], in1=rstd, op=ALU.mult),
                 [Y.t(ch * 2 + b), rres], [tmp.t(k)])
            c.dv('scalar_tensor_tensor', dict(out=H[ch], in0=tmp[k], scalar=gp, in1=H[ch], op0=ALU.mult, op1=ALU.add),
                 [H.t(ch), tmp.t(k), g["gains"].t()], [H.t(ch)])
        if not final:
            c.dma("sp", c.dr["hT"][:, :, b * 512:(b + 1) * 512], H.ap, [H.t()], [hT.t(b)])
        if pre_key is not None:
            rstd2, rres2 = rms_stats(c, lambda ch: H[ch], lambda ch: H.t(ch), 512)
            rms_apply(c, lambda ch: H[ch], lambda ch: H.t(ch), pre_key, pre_layer, rstd2, rres2,
                      lambda ch: hn_out[ch * 2 + b], lambda ch: hn_out.t(ch * 2 + b))
        if final:
            od = c.dten("out", 8)
            for tt in range(4):
                ob = tt % 2
                for g4 in range(4):
                    bk = g4
                    for k in range(4):
                        ch = g4 * 4 + k
                        c.tr(c.pb(bk)[:, k * 128:(k + 1) * 128], H[ch][:, tt * 128:(tt + 1) * 128], g["ident"][0],
                             [H.t(ch), g["ident"].t()], [c.bank(bk)])
                    c.copy(otok[ob][:, g4 * 512:(g4 + 1) * 512], c.pb(bk), [c.bank(bk)], [otok.t(ob)])
                row = (b * 4 + tt) * 128
                tok = c.dma("sp", c.dr["out"][row:row + 128, :], otok[ob], [otok.t(ob)], [od.t(b * 4 + tt)])
                g.setdefault("final", []).append(tok)
    c.top = mark


def mem_kv(c, layer):
    g = c.g
    kmT, vM = g["kmT"], g["vM"]
    c.dv('memset', dict(ap=vM.ap[:, :, :, 128:130], constant=1.0), [], [vM.t()])
    mark = c.top
    xtok = c.alloc("mxtok", 2, [2048], F32)
    mT = c.alloc("mT", 16, [256], F32)
    mn = c.alloc("mnT", 16, [256], BF16)
    mem = c.dr["mem"]
    transpose_in(c, lambda e: mem[e * 128:(e + 1) * 128, :], 2,
                 lambda g4, e: mT.ap[:, g4 * 4:g4 * 4 + 4, e * 128:(e + 1) * 128],
                 lambda g4, e: [mT.t(g4 * 4, g4 * 4 + 4)], xtok)
    rstd, rres = rms_stats(c, lambda ch: mT[ch], lambda ch: mT.t(ch), 256)
    rms_apply(c, lambda ch: mT[ch], lambda ch: mT.t(ch), "mem_norm", layer, rstd, rres,
              lambda ch: mn[ch], lambda ch: mn.t(ch))
    for t in range(512 // WCOLS):
        spec = ("mem_wk", layer, 0, 16, t * WCOLS, WCOLS)

        def evac(j, b, bk):
            h = t * (WCOLS // 128) + j
            c.copy(kmT[h], c.pb(bk)[:, 0:256], [c.bank(bk)], [kmT.t(h)])
        proj_fm(c, spec, 16, lambda kc, b: (mn[kc], [mn.t(kc)]), 1, evac)
    for t in range(512 // WCOLS):
        spec = ("mem_wv", layer, 0, 16, t * WCOLS, WCOLS)
        w = c.wget(spec)
        for e in range(2):
            def evac(bk, e=e, t=t):
                nh = WCOLS // 128
                c.copy(vM.ap[:, e, t * nh:(t + 1) * nh, 0:128], c.pb(bk)[:, 0:WCOLS].rearrange("p (h d) -> p h d", h=nh),
                       [c.bank(bk)], [vM.t(e)])
            proj_tm(c, w, 16, lambda kc, e=e: (mn[kc][:, e * 128:(e + 1) * 128], [mn.t(kc)]), evac)
    c.top = mark


def mem_attn(c, layer, hn, Y):
    g = c.g
    mark = c.top
    kmT, vM = g["kmT"], g["vM"]
    qm = c.alloc("qmT", 4 * 2, [512], BF16)
    oc = c.alloc("ocT", 4 * 2, [512], BF16)
    pt = c.alloc("mpt", 2, [2, 512], BF16)
    den = c.alloc("mden", 2, [8], F32)
    on = c.alloc("mon", 2, [128], F32)
    for t in range(512 // WCOLS):
        spec = ("mem_wq", layer, 0, 16, t * WCOLS, WCOLS)

        def evac(j, b, bk):
            h = t * (WCOLS // 128) + j
            c.copy(qm[h * 2 + b], c.pb(bk), [c.bank(bk)], [qm.t(h * 2 + b)])
        proj_fm(c, spec, 16, lambda kc, b: (hn[kc * 2 + b], [hn.t(kc * 2 + b)]), 2, evac)
    den = c.alloc("mden8", 8, [8], F32)
    on = c.alloc("mon8", 8, [128], F32)
    items = [(h, b) for h in range(4) for b in range(2)]

    def stA(i):
        h, b = items[i]
        k = i % 2
        for kt in range(2):
            c.mm(c.pb(2 * k + kt), kmT[h][:, kt * 128:(kt + 1) * 128], qm[h * 2 + b], True, True,
                 [kmT.t(h), qm.t(h * 2 + b)], [c.bank(2 * k + kt)])
        c.act(pt[k].rearrange("p a n -> p (a n)"), c.pb2(2 * k, 1024), AF.Exp, [c.bank(2 * k, 2 * k + 2)], [pt.t(k)],
              scale=float(SCALE))

    def stB(i):
        h, b = items[i]
        k = i % 2
        for qt in range(4):
            ab = 4 + (qt % 2)
            for kt in range(2):
                c.mm(c.pb(ab)[:, 0:129], pt[k][:, kt, qt * 128:(qt + 1) * 128], vM.ap[:, kt, h, 0:129], kt == 0, kt == 1,
                     [pt.t(k), vM.t(kt)], [c.bank(ab)])
            tail_norm(c, ab, 128, None, den, on, k * 4 + qt)

    def stC(i):
        h, b = items[i]
        k = i % 2
        for qt in range(4):
            tail_tr(c, on, k * 4 + qt, oc[h * 2 + b][:, qt * 128:(qt + 1) * 128], [oc.t(h * 2 + b)], 6 + qt % 2)

    pipeline(len(items), [stA, stB, stC])
    proj_Y(c, "mem_wo", layer, 4, lambda kc, b: (oc[kc * 2 + b], [oc.t(kc * 2 + b)]), Y)
    c.top = mark


def mlp(c, layer, hn, Y):
    g = c.g
    mark = c.top
    if layer == 0:
        c.ensure_mats([(m[0], m[1]) for m in WMATS[8:]])

    uT = c.alloc("uT", 16 * 2, [512], BF16)
    rl = c.alloc("relu", 2, [512], F32)
    yf = Y.flat.rearrange("p (c t) -> p c t", c=16)
    ri = [0]
    for qd in range(4):
        for t in range(2048 // WCOLS):
            spec = ("mlp_w_up", layer, 0, 16, qd * 2048 + t * WCOLS, WCOLS)

            def evac(j, b, bk):
                jj = t * (WCOLS // 128) + j
                k = ri[0] % 2
                ri[0] += 1
                c.act(rl[k], c.pb(bk), AF.Relu, [c.bank(bk)], [rl.t(k)])
                c.dv('tensor_tensor', dict(out=uT[jj * 2 + b], in0=rl[k], in1=rl[k], op=ALU.mult), [rl.t(k)],
                      [uT.t(jj * 2 + b)])
            proj_fm(c, spec, 16, lambda kc, b: (hn[kc * 2 + b], [hn.t(kc * 2 + b)]), 2, evac)
        for t in range(2048 // WCOLS):
            spec = ("mlp_w_down", layer, qd * 2048, 16, t * WCOLS, WCOLS)

            def evac(j, b, bk):
                n = t * (WCOLS // 128) + j
                dst = yf[:, n, b * 512:(b + 1) * 512]
                if qd == 0:
                    c.copy(dst, c.pb(bk), [c.bank(bk)], [Y.t(n * 2 + b)])
                else:
                    c.dv('tensor_tensor', dict(out=dst, in0=dst, in1=c.pb(bk), op=ALU.add),
                          [c.bank(bk), Y.t(n * 2 + b)], [Y.t(n * 2 + b)])
            proj_fm(c, spec, 16, lambda kc, b: (uT[kc * 2 + b], [uT.t(kc * 2 + b)]), 2, evac)
    c.top = mark


def layer_tail(c, layer, mix_src, mix_w, final):
    g = c.g
    c.top = g["base"]
    g["kmT"] = c.alloc("kmT", 4, [256], BF16)
    g["vM"] = c.alloc("vM", 2, [4, 130], BF16)
    Y = c.alloc("Y", 16 * 2, [512], F32)
    hn = c.alloc("hn", 16 * 2, [512], BF16)
    g["hn"] = hn
    base2 = c.top
    assert base2 <= g["base"] + CAT_OFF
    src_alloc = mix_src(c)
    proj_Y(c, mix_w, 0, 16, lambda kc, b: (src_alloc[kc * 2 + b], [src_alloc.t(kc * 2 + b)]), Y)
    c.top = base2
    mem_kv(c, layer)
    post_pre(c, Y, layer, "mix_post", "mem_pre", layer, hn)
    mem_attn(c, layer, hn, Y)
    post_pre(c, Y, layer, "mem_post", "mlp_pre", layer, hn)
    mlp(c, layer, hn, Y)
    if final:
        post_pre(c, Y, layer, "mlp_post", None, None, None, final=True)
    else:
        post_pre(c, Y, layer, "mlp_post", "mix_pre", layer + 1, hn)


LAMBDA_INIT1 = 0.8 - 0.6 * math.exp(-0.3 * 1)


def l1_inproj(c, gather=False):
    g = c.g
    hn = g["hn"]
    c.top = g["base"]
    rp = c.alloc("rope", 2, [1280], F32)
    g["rope"] = rp
    c.dma("sp", rp.ap[0:32], c.dr["rope"], [], [rp.t()])
    g["ropetmp"] = c.alloc("ropetmp", 4, [512], F32)
    stg = c.alloc("stg1", 4, [512], BF16)
    vst = c.alloc("vst", 3, [258], BF16)
    c.dv('memset', dict(ap=vst.ap[:, :, 256:258], constant=1.0), [], [vst.t()])
    qT = c.alloc("qT1", 16 * 2, [512], BF16, at=g["base"] + CAT_OFF)
    g["qT1"] = qT
    kd = c.dten("kloc", 16)
    vd = c.dten("vloc", 8)
    si = [0]
    src = lambda kc, b: (hn[kc * 2 + b], [hn.t(kc * 2 + b)])
    for t in range(2048 // WCOLS):
        spec = ("odd_w_in", 0, 0, 16, 2048 + t * WCOLS, WCOLS)

        def evac(j, b, bk):
            hp = t * (WCOLS // 128) + j
            k = si[0] % 4
            si[0] += 1
            c.copy(stg[k], c.pb(bk), [c.bank(bk)], [stg.t(k)])
            rope_fm(c, stg[k], [stg.t(k)], 512, 128 + b * 512)
            c.dma("sp", c.dr["kloc"][hp * 128:(hp + 1) * 128, b * 512:(b + 1) * 512], stg[k], [stg.t(k)], [kd.t(hp)])
        proj_fm(c, spec, 16, src, 2, evac)
    vi = [0]
    for h in range(8):
        spec = ("odd_w_in", 0, 0, 16, 4096 + h * 256, 256)
        w = c.wget(spec)
        for e in range(8):
            def evac(bk, e=e, h=h):
                k = vi[0] % 3
                vi[0] += 1
                c.copy(vst[k][:, 0:256], c.pb(bk)[:, 0:256], [c.bank(bk)], [vst.t(k)])
                c.dma("sp", c.dr["vloc"][e * 128:(e + 1) * 128, h * 258:(h + 1) * 258], vst[k], [vst.t(k)], [vd.t(e)])
            proj_tm(c, w, 16, lambda kc, e=e: (hn[kc * 2 + e // 4][:, (e % 4) * 128:(e % 4 + 1) * 128],
                                               [hn.t(kc * 2 + e // 4)]), evac)
    if gather:
        l1_gather(c, "k")
        l1_gather(c, "v")
    for t in range(2048 // WCOLS):
        spec = ("odd_w_in", 0, 0, 16, t * WCOLS, WCOLS)

        def evac(j, b, bk):
            hp = t * (WCOLS // 128) + j
            c.copy(qT[hp * 2 + b], c.pb(bk), [c.bank(bk)], [qT.t(hp * 2 + b)])
            rope_fm(c, qT[hp * 2 + b], [qT.t(hp * 2 + b)], 512, 128 + b * 512)
        proj_fm(c, spec, 16, src, 2, evac)


def l1_gather(c, which):
    grp = [list(range(NCORES))]
    loc, full, nt = ("kloc", "kfull", 16) if which == "k" else ("vloc", "vfull", 8)
    dl, dfu = c.dten(loc, nt), c.dten(full, 1)
    i_ap, o_ap = c.dr[loc], c.dr[full]
    c.s.op("pool", lambda e: e.collective_compute("AllGather", ALU.bypass, replica_groups=grp, ins=[i_ap.opt()],
                                                  outs=[o_ap.opt()]), [dl.t()], [dfu.t()], dma="cc")


def l1_attn(c):
    g = c.g
    qT = g["qT1"]
    c.top = g["base"]
    kq = c.alloc("kq", 5, [2, 2048], BF16)
    vq = c.alloc("vq", 5, [16, 258], BF16)
    pt = c.alloc("pt1", 3, [512], BF16)
    lv = c.alloc("lamv", 4, [128], F32)
    lt = c.alloc("lamt", 8, [2], F32)
    sg = c.alloc("sgs", 1, [256], F32)
    rr = c.alloc("rr", 4, [4], F32)
    tmp = c.alloc("otmp", 2, [256], F32)
    ob = c.alloc("obuf", 2, [256], F32)
    on = c.alloc("onrm", 4, [256], F32)
    jk = c.alloc("junk", 1, [256], F32)
    ost = c.alloc("ost", 2, [2, 256], BF16)
    kf, vf = c.dten("kfull", 1), c.dten("vfull", 1)
    cd = c.dten("cat1", 16)
    c.dma("sp", lv.ap, c.dr["lamvec"], [], [lv.t()])
    c.dma("sp", sg[0], c.dr["subln"], [], [sg.t()])
    c.dv('tensor_scalar', dict(out=sg[0], in0=sg[0], scalar1=float((1.0 - LAMBDA_INIT1) * 16.0), scalar2=None,
                               op0=ALU.mult), [sg.t()], [sg.t()])
    for i in range(2):
        c.dv('tensor_tensor', dict(out=lv[2 * i], in0=lv[2 * i], in1=lv[2 * i + 1], op=ALU.mult),
             [lv.t(2 * i, 2 * i + 2)], [lv.t(2 * i)])
        c.dv('reduce_sum', dict(out=lt[i][:, 0:1], in_=lv[2 * i], axis=AX.X), [lv.t(2 * i)], [lt.t(i)])
        c.act(lt[i][:, 0:1], lt[i][:, 0:1], AF.Exp, [lt.t(i)], [lt.t(i)])
    c.dv('tensor_tensor', dict(out=lt[2][:, 0:1], in0=lt[1][:, 0:1], in1=lt[0][:, 0:1], op=ALU.subtract),
         [lt.t(0, 2)], [lt.t(2)])
    c.dv('tensor_scalar', dict(out=lt[2][:, 0:1], in0=lt[2][:, 0:1], scalar1=float(-LAMBDA_INIT1), scalar2=None,
                               op0=ALU.add), [lt.t(2)], [lt.t(2)])
    nlam = lt[2][:, 0:1]

    kfull = c.dr["kfull"].rearrange("(r hp d) t -> d hp r t", r=NCORES, hp=16)
    vfull = c.dr["vfull"].rearrange("(kt p) c -> p kt c", p=128)

    def load_quarter(n):
        h, i = divmod(n, 4)
        buf = n % 5
        for t in range(2):
            c.dma("sp", kq[buf][:, t].rearrange("p (r t) -> p r t", r=2), kfull[:, 2 * h + t, 2 * i:2 * i + 2, :],
                  [kf.t()], [kq.t(buf)])
        c.dma("sp", vq[buf], vfull[:, i * 16:(i + 1) * 16, h * 258:(h + 1) * 258], [vf.t()], [vq.t(buf)])

    accs = c.alloc("accs", 8, [258], F32)
    assert c.top <= g["base"] + CAT_OFF, c.top - g["base"]
    for n in range(5):
        load_quarter(n)
    steps = [(h, qb, kt) for h in range(8) for qb in range(4) for kt in range(64)]
    pending = []

    def stA(s):
        h, qb, kt = steps[s]
        n = h * 4 + kt // 16
        buf = n % 5
        sbk, k3 = s % 2, s % 3
        for t in range(2):
            qa = qT[(2 * h + t) * 2 + qb // 2][:, (qb % 2) * 256:(qb % 2) * 256 + 256]
            c.mm(c.pb(sbk)[:, t * 256:(t + 1) * 256], kq[buf][:, t, (kt % 16) * 128:(kt % 16 + 1) * 128], qa,
                 True, True, [kq.t(buf), qT.t((2 * h + t) * 2 + qb // 2)], [c.bank(sbk)])
        c.act(pt[k3], c.pb(sbk), AF.Exp, [c.bank(sbk)], [pt.t(k3)], scale=float(SCALE))

    def stB(s):
        h, qb, kt = steps[s]
        n = h * 4 + kt // 16
        buf = n % 5
        k3 = s % 3
        for t in range(2):
            for qt in range(2):
                ab = 2 + t * 2 + qt
                c.mm(c.pb(ab)[:, 0:257], pt[k3][:, t * 256 + qt * 128:t * 256 + qt * 128 + 128],
                     vq[buf][:, kt % 16, 0:257], kt == 0, kt == 63, [pt.t(k3), vq.t(buf)], [c.bank(ab)])
        if qb == 3 and kt % 16 == 15 and n + 5 < 32:
            load_quarter(n + 5)
        if kt == 63:
            epilogue(h, qb, s)
        while pending and (pending[0][0] <= s or s == len(steps) - 1):
            pending.pop(0)[1]()

    def epilogue(h, qb, s_now):
        hq = h * 4 + qb
        ko = hq % 2
        for j in range(4):
            c.copy(accs[ko * 4 + j][:, 0:257], c.pb(2 + j)[:, 0:257], [c.bank(2 + j)], [accs.t(ko * 4 + j)], eng="dve")
        for qt in range(2):
            k = ko * 2 + qt
            a0, a1 = accs[ko * 4 + qt], accs[ko * 4 + 2 + qt]
            r0, r1 = accs.t(ko * 4 + qt), accs.t(ko * 4 + 2 + qt)
            c.dv('reciprocal', dict(out=rr[k][:, 0:1], in_=a0[:, 256:257]), [r0], [rr.t(k)])
            c.dv('reciprocal', dict(out=rr[k][:, 1:2], in_=a1[:, 256:257]), [r1], [rr.t(k)])
            c.dv('tensor_tensor', dict(out=rr[k][:, 2:3], in0=rr[k][:, 1:2], in1=nlam, op=ALU.mult),
                 [rr.t(k), lt.t(2)], [rr.t(k)])
            c.dv('tensor_scalar', dict(out=tmp[qt], in0=a0[:, 0:256], scalar1=rr[k][:, 0:1], scalar2=None, op0=ALU.mult),
                 [r0, rr.t(k)], [tmp.t(qt)])
            c.dv('scalar_tensor_tensor', dict(out=ob[qt], in0=a1[:, 0:256], scalar=rr[k][:, 2:3], in1=tmp[qt],
                                              op0=ALU.mult, op1=ALU.add), [r1, rr.t(k), tmp.t(qt)], [ob.t(qt)])
            c.dv('tensor_tensor', dict(out=jk[0], in0=ob[qt], in1=ob[qt], op=ALU.mult), [ob.t(qt)], [jk.t()])
            c.dv('reduce_sum', dict(out=rr[k][:, 3:4], in_=jk[0], axis=AX.X), [jk.t()], [rr.t(k)])
            c.act(rr[k][:, 3:4], rr[k][:, 3:4], AF.Ln, [rr.t(k)], [rr.t(k)], bias=float(256 * EPS), scale=1.0)
            c.act(rr[k][:, 3:4], rr[k][:, 3:4], AF.Exp, [rr.t(k)], [rr.t(k)], scale=-0.5)
            c.dv('scalar_tensor_tensor', dict(out=on[k], in0=ob[qt], scalar=rr[k][:, 3:4], in1=sg[0], op0=ALU.mult,
                                              op1=ALU.mult), [ob.t(qt), rr.t(k), sg.t()], [on.t(k)])
        pending.append((s_now + 6, lambda: epilogue2(h, qb)))

    def epilogue2(h, qb):
        hq = h * 4 + qb
        ko = hq % 2
        for qt in range(2):
            k = ko * 2 + qt
            for dvc in range(2):
                tb = 6 + dvc
                c.tr(c.pb(tb)[:, 0:128], on[k][:, dvc * 128:(dvc + 1) * 128], g["ident"][0],
                     [on.t(k), g["ident"].t()], [c.bank(tb)])
                c.copy(ost[ko][:, dvc, qt * 128:(qt + 1) * 128], c.pb(tb)[:, 0:128], [c.bank(tb)], [ost.t(ko)], eng="dve")
        c.dma("sp", c.dr["cat1"][:, 2 * h:2 * h + 2, qb * 256:(qb + 1) * 256], ost[ko], [ost.t(ko)],
              [cd.t(2 * h, 2 * h + 2)])

    pipeline(len(steps), [stA, stB])


def l1_mix_src(c):
    g = c.g
    cat = c.alloc("catT1", 16 * 2, [512], BF16, at=g["base"] + CAT_OFF)
    cd = c.dten("cat1", 16)
    c.dma("sp", cat.flat.rearrange("p (c t) -> p c t", c=16), c.dr["cat1"], [cd.t()], [cat.t()])
    return cat


STOP = None
DEBUG = False


def program(c, mode):
    g = c.g
    load_consts(c)
    c.ensure_mats([(m[0], m[1]) for m in (WMATS[:6] if mode != "B" else WMATS[8:])])
    if mode in ("A", "fused"):
        l0_front(c)
        if STOP == "front":
            c.dump("hnT", g["hnT"].flat, [128, 16 * 1536], BF16, [g["hnT"].t()])
            return c.s.dry or c.s.all_tokens_wait("sp", list(c.s.dma_cnt.items()))
        l0_inproj(c)
        if STOP == "inproj":
            c.dump("vA", g["vA"].flat, [128, 12 * 8 * 130], BF16, [g["vA"].t()])
            c.dump("vS", g["vS"].flat, [128, 10 * 2 * 130], BF16, [g["vS"].t()])
            return c.s.dry or c.s.all_tokens_wait("sp", list(c.s.dma_cnt.items()))
        l0_attn(c)
        if STOP == "attn":
            c.dump("catT", g["catT"].flat, [128, 16 * 1024], BF16, [g["catT"].t()])
            return c.s.dry or c.s.all_tokens_wait("sp", list(c.s.dma_cnt.items()))
        layer_tail(c, 0, lambda cc: cc.g["catT"], "even_w_out", final=False)
        if STOP == "l0":
            return c.s.dry or c.s.all_tokens_wait("sp", list(c.s.dma_cnt.items()))
        l1_inproj(c, gather=(mode == "fused"))
        if mode == "A":
            qT = g["qT1"]
            c.dma("sp", c.dr["qT1_d"], qT.ap, [qT.t()], [c.dten("qT1_d").t()])
    if mode in ("B", "fused"):
        if mode == "B":
            hT = c.dten("hT", 2)
            c.dma("sp", c.dr["hT"], c.dr["hT_in"], [], [hT.t()])
            qT = c.alloc("qT1", 16 * 2, [512], BF16, at=g["base"] + CAT_OFF)
            g["qT1"] = qT
            c.dma("sp", qT.ap, c.dr["qT1_d"], [], [qT.t()])
        l1_attn(c)
        layer_tail(c, 1, l1_mix_src, "odd_w_out", final=True)
    if not c.s.dry:
        c.s.all_tokens_wait("sp", list(c.s.dma_cnt.items()))


def build(mode, debug=None):
    from contextlib import ExitStack
    nc = bass.Bass("TRN2", target_bir_lowering=False)
    dr = {}

    def ten(name, shape, dt=F32, kind="ExternalInput"):
        if DEBUG and kind == "Internal" and name in ("qaT", "kaT", "qsT", "ksT", "cat1", "hT"):
            kind = "ExternalOutput"
        dr[name] = nc.dram_tensor(name, list(shape), dt, kind=kind).ap()

    for nm, shp in (("ident", [128, 128]), ("perm", [32, 32]), ("gains", [128, 224]), ("rope", [32, 2, 1280]),
                    ("mem", [256, 2048]), ("wshard", [WTOT8])):
        ten(nm, shp)
    ten("wl", [WTOT8], BF16, "Internal")
    for (wn, wlayer, wK, wN) in WMATS:
        ten("wf_%s%d" % (wn, wlayer), [wK * wN // (min(wK, 2048) * 2), min(wK, 2048) * 2], BF16, "Internal")
    A, B = mode in ("A", "fused"), mode in ("B", "fused")
    if A:
        for nm, shp in (("x_ext", [1536, 2048]), ("na_bias", [8, 8, 128, 768]), ("sw_mask", [128, 8, 384]),
                        ("sinks", [128, 8])):
            ten(nm, shp)
        for nm, shp in (("qaT", [8, 128, 1024]), ("kaT", [8, 128, 1536]), ("qsT", [8, 128, 1024]), ("ksT", [2, 128, 1280])):
            ten(nm, shp, BF16, "Internal")
    if B:
        for nm, shp in (("lamvec", [128, 4, 128]), ("subln", [128, 256])):
            ten(nm, shp)
        ten("cat1", [128, 16, 1024], BF16, "Internal")
        ten("out", [1024, 2048], F32, "ExternalOutput")
    ext_o = "ExternalOutput" if mode == "A" else "Internal"
    ext_i = "ExternalInput" if mode == "B" else "Internal"
    ten("hT", [128, 16, 1024], F32, ext_o)
    if mode == "B":
        ten("hT_in", [128, 16, 1024], F32, "ExternalInput")
    if A:
        ten("kloc", [2048, 1024], BF16, ext_o)
        ten("vloc", [1024, 2064], BF16, ext_o)
    if mode != "fused":
        ten("qT1_d", [128, 32, 512], BF16, "ExternalOutput" if mode == "A" else "ExternalInput")
    if B:
        ten("kfull", [NCORES * 2048, 1024], BF16, ext_i)
        ten("vfull", [NCORES * 1024, 2064], BF16, ext_i)
    with ExitStack() as st:
        sb = st.enter_context(nc.sbuf_tensor("arena", [128, Ctx.SB_BYTES // 4], F32))
        ps = st.enter_context(nc.psum_tensor("psum", [128, 4096], F32))
        cdry = Ctx(nc, sb, ps, dr, dry=True)
        program(cdry, mode)
        c = Ctx(nc, sb, ps, dr, wplan=cdry.wrec)
        c.debug = debug
        program(c, mode)
        sems = {n: st.enter_context(nc.semaphore(n)) for n in c.sem_names()}
        block = st.enter_context(nc.Block())
        c.s.emit(nc, sems, block)
    return nc, c


def _bf16():
    import ml_dtypes
    return ml_dtypes.bfloat16


def _host_consts(inputs):
    f32 = np.float32
    gl = []
    for k in ("mix_pre_g", "mix_post_g", "mem_norm_g", "mem_pre_g", "mem_post_g", "mlp_pre_g", "mlp_post_g"):
        for l in range(2):
            gl.append(np.asarray(inputs[k][l], f32).reshape(16, 128).T)
    gains = np.ascontiguousarray(np.stack(gl, axis=1).reshape(128, 224))
    perm = np.zeros((32, 32), f32)
    for m in range(32):
        perm[(m + 16) % 32, m] = 1.0
    com = {"ident": np.eye(128, dtype=f32), "perm": perm, "gains": gains,
           "mem": np.ascontiguousarray(inputs["mem"][0], f32)}
    return com


def _tiled(inputs, n, l, K, N):
    w = np.asarray(inputs[n][l], np.float32)
    nk = min(K, 2048) // 128
    wt = w.reshape(K // (nk * 128), nk, 128, N // WCOLS, WCOLS).transpose(0, 3, 2, 1, 4)
    return np.ascontiguousarray(wt).reshape(NCORES, -1)


def _wshards(inputs):
    tl = [_tiled(inputs, n, l, K, N) for (n, l, K, N) in WMATS]
    return [np.concatenate([t[r] for t in tl]) for r in range(NCORES)]


def _rope_table(start):
    f32 = np.float32
    inv = (1.0 / (f32(500000.0) ** (np.arange(0, 32, 2, dtype=f32) / f32(32)))).astype(f32)
    pos = (start - 128 + np.arange(1280)).astype(f32)
    ang = (pos[None, :] * inv[:, None]).astype(f32)
    cs, sn = np.cos(ang).astype(f32), np.sin(ang).astype(f32)
    tab = np.zeros((32, 2, 1280), f32)
    tab[0:16, 0], tab[16:32, 0] = cs, cs
    tab[0:16, 1], tab[16:32, 1] = -sn, sn
    return tab


def _na_bias(rpb, core):
    out = np.full((8, 8, 128, 6, 128), NEG, np.float32)
    t = np.arange(128)
    for a in range(8):
        G = 8 * core + a
        e0 = a if a < 6 else a - 1
        qi = 2 * G + t // 64
        qj = t % 64
        rs = np.clip(qi - 4, 0, 120)
        cs = np.clip(qj - 8, 0, 48)
        for j in range(6):
            Gk = 8 * core - 2 + e0 + j
            if Gk < 0 or Gk >= 64:
                continue
            kr = 2 * Gk + t // 64
            kc = t % 64
            valid = ((kr[:, None] >= rs[None, :]) & (kr[:, None] < rs[None, :] + 8) &
                     (kc[:, None] >= cs[None, :]) & (kc[:, None] < cs[None, :] + 16))
            ri = np.clip(kr[:, None] - qi[None, :] + 7, 0, 14)
            ci = np.clip(kc[:, None] - qj[None, :] + 15, 0, 30)
            vals = rpb[:, ri, ci]
            out[a, :, :, j, :] = np.where(valid[None], vals, np.float32(NEG))
    return out.reshape(8, 8, 128, 768)


def _sw_mask(core):
    m = np.zeros((128, 8, 3, 128), np.float32)
    kk = np.arange(128)[:, None]
    qq = np.arange(128)[None, :]
    for a in range(8):
        G = 8 * core + a
        m[:, a, 0, :] = np.where((qq <= kk) & (G - 1 >= 0), 0.0, NEG)
        m[:, a, 2, :] = np.where((kk <= qq) & (G + 1 < 64), 0.0, NEG)
    return m.reshape(128, 8, 384)


_PROGS = {}


def _get(mode):
    if mode not in _PROGS:
        _PROGS[mode] = build(mode)[0]
    return _PROGS[mode]


def _maps_A(inputs, com):
    f32 = np.float32
    x = np.asarray(inputs["x"][0], f32)
    xp = np.zeros((S + 256 + 256 + 256, D), f32)
    xp[256:256 + S] = x
    rpb = np.asarray(inputs["na_rpb"][0], f32)
    ws = _wshards(inputs)
    maps = []
    for cidx in range(NCORES):
        start = cidx * T
        m = dict(com)
        m["x_ext"] = np.ascontiguousarray(xp[start:start + 1536])
        m["na_bias"] = _na_bias(rpb, cidx)
        m["sw_mask"] = _sw_mask(cidx)
        m["sinks"] = np.ascontiguousarray(np.broadcast_to(np.asarray(inputs["sw_sinks"][0], f32)[None, :], (128, 8)))
        m["rope"] = _rope_table(start)
        m["wshard"] = ws[cidx]
        maps.append(m)
    return maps


def _maps_B_extra(inputs, m):
    f32 = np.float32
    lv = np.stack([np.asarray(inputs[k][0], f32) for k in ("diff_lam_q1", "diff_lam_k1", "diff_lam_q2", "diff_lam_k2")])
    m["lamvec"] = np.ascontiguousarray(np.broadcast_to(lv[None], (128, 4, 128)))
    m["subln"] = np.ascontiguousarray(np.broadcast_to(np.asarray(inputs["diff_subln_g"][0], f32)[None], (128, 256)))


MODE = "fused"


def kernel(**inputs):
    com = _host_consts(inputs)
    cores = list(range(NCORES))
    if MODE == "fused":
        maps = _maps_A(inputs, com)
        for m in maps:
            _maps_B_extra(inputs, m)
        res = run_bass_kernel_spmd(_get("fused"), maps, core_ids=cores)
        outs = [r["out"] for r in res.results]
    else:
        mapsA = _maps_A(inputs, com)
        resA = run_bass_kernel_spmd(_get("A"), mapsA, core_ids=cores).results
        kfull = np.concatenate([r["kloc"] for r in resA], axis=0)
        vfull = np.concatenate([r["vloc"] for r in resA], axis=0)
        mapsB = []
        for cidx in range(NCORES):
            m = dict(com)
            m["wshard"] = mapsA[cidx]["wshard"]
            m["rope"] = mapsA[cidx]["rope"]
            _maps_B_extra(inputs, m)
            m["hT_in"] = resA[cidx]["hT"]
            m["qT1_d"] = resA[cidx]["qT1_d"]
            m["kfull"] = kfull
            m["vfull"] = vfull
            mapsB.append(m)
        resB = run_bass_kernel_spmd(_get("B"), mapsB, core_ids=cores).results
        outs = [r["out"] for r in resB]
    return np.concatenate(outs, axis=0).reshape(1, S, D).astype(np.float32)
```

```python
import math
import numpy as np
import concourse.bass as bass
import concourse.mybir as mybir
from concourse.bass_utils import run_bass_kernel_spmd

F32 = mybir.dt.float32
BF16 = mybir.dt.bfloat16
AF = mybir.ActivationFunctionType
ALU = mybir.AluOpType
AX = mybir.AxisListType

NCORES = 8
S = 8192
D = 2048
DC = 16
T = 1024
NEG = -30000.0
EPS = 1e-6
SCALE = 128 ** -0.5

ENGS = ("pe", "act", "dve", "pool", "sp")


def _prod(xs):
    r = 1
    for x in xs:
        r *= x
    return r


class Alloc:
    def __init__(self, space, name, lo, ntiles, tile_bytes, ap):
        self.space, self.name, self.lo = space, name, lo
        self.ntiles, self.tb = ntiles, tile_bytes
        self.hi = lo + ntiles * tile_bytes
        self.ap = ap
        self.ovl = []
        self.w = {}
        self.r = {}

    def __getitem__(self, i):
        return self.ap[:, i]

    def t(self, i=None, j=None):
        if i is None:
            return (self, 0, self.ntiles)
        return (self, i, (i + 1) if j is None else j)


class Sched:
    def __init__(self):
        self.ops = {e: [] for e in ENGS}
        self.seen = {e: {} for e in ENGS}
        self.allocs = []
        self.dma_cnt = {}
        self.qsems = {}
        self.qnext = {}
        self.dry = False

    def register(self, a):
        if a.space in ("sb", "ps"):
            for b in self.allocs:
                if b.space == a.space and b.lo < a.hi and a.lo < b.hi:
                    a.ovl.append(b)
                    b.ovl.append(a)
        self.allocs.append(a)
        return a

    def _tiles(self, ref):
        a, i0, i1 = ref
        for i in range(i0, i1):
            yield a, i
        if a.ovl:
            lo = a.lo + i0 * a.tb
            hi = a.lo + i1 * a.tb
            for b in a.ovl:
                j0 = max(0, (lo - b.lo) // b.tb)
                j1 = min(b.ntiles, -((b.lo - hi) // b.tb))
                for j in range(j0, j1):
                    yield b, j

    def op(self, eng, fn, reads=(), writes=(), dma=None):
        if self.dry:
            return None
        deps = {}

        def add(tok, raw):
            src, val = tok
            if src == eng and not dma and (eng == "pe" or not raw):
                return
            if val > deps.get(src, 0):
                deps[src] = val

        for ref in reads:
            for a, i in self._tiles(ref):
                w = a.w.get(i)
                if w is not None:
                    add(w, True)
        for ref in writes:
            for a, i in self._tiles(ref):
                w = a.w.get(i)
                if w is not None:
                    add(w, False)
                rr = a.r.get(i)
                if rr:
                    for src, val in rr.items():
                        add((src, val), False)
        ops = self.ops[eng]
        idx = len(ops) + 1
        waits = []
        seen = self.seen[eng]
        if dma:
            names = self.qsems[dma]
            k = self.qnext[dma]
            self.qnext[dma] = (k + 1) % len(names)
            sem = names[k]
            inc = 1 if dma == "cc" else 16
            prev = self.dma_cnt.get(sem, 0)
            if prev:
                add((sem, prev), True)
            self.dma_cnt[sem] = prev + inc
            tok = (sem, prev + inc)
        else:
            sem = None
            tok = (eng, idx)
        for src, val in deps.items():
            if seen.get(src, 0) >= val:
                continue
            seen[src] = val
            waits.append((src, val))
            if src in self.ops:
                self.ops[src][val - 1]["sig"] = True
        ops.append({"fn": fn, "waits": waits, "sig": False, "dsem": sem, "dinc": 1 if dma == "cc" else 16})
        for ref in reads:
            a, i0, i1 = ref
            for i in range(i0, i1):
                a.r.setdefault(i, {})[tok[0]] = tok[1]
        for ref in writes:
            a, i0, i1 = ref
            for i in range(i0, i1):
                a.w[i] = tok
                a.r[i] = {}
        return tok

    def all_tokens_wait(self, eng, toks):
        ops = self.ops[eng]
        waits = []
        for src, val in toks:
            waits.append((src, val))
            if src in self.ops:
                self.ops[src][val - 1]["sig"] = True
        ops.append({"fn": None, "waits": waits, "sig": False, "dsem": None})

    def emit(self, nc, sems, block):
        cum = {}
        for e in ENGS:
            c = 0
            lst = []
            for o in self.ops[e]:
                if o["sig"]:
                    c += 1
                lst.append(c)
            cum[e] = lst
        handles = {"pe": block.tensor, "act": block.scalar, "dve": block.vector,
                   "pool": block.gpsimd, "sp": block.sync}

        def make(e):
            def body(eng):
                for o in self.ops[e]:
                    for src, val in o["waits"]:
                        if src in self.ops:
                            eng.wait_ge(sems[src], cum[src][val - 1])
                        else:
                            eng.wait_ge(sems[src], val)
                    if o["fn"] is None:
                        continue
                    ins = o["fn"](eng)
                    if o["dsem"] is not None:
                        ins.then_inc(sems[o["dsem"]], o["dinc"])
                    if o["sig"]:
                        ins.then_inc(sems[e], 1)
            return body

        for e in ENGS:
            if self.ops[e]:
                handles[e](make(e))


WCOLS = 256
WLOOK = 12
WMATS = [("even_w_in", 0, 2048, 4608), ("even_w_out", 0, 2048, 2048),
         ("mem_wk", 0, 2048, 512), ("mem_wv", 0, 2048, 512), ("mem_wq", 0, 2048, 512), ("mem_wo", 0, 512, 2048),
         ("mlp_w_up", 0, 2048, 8192), ("mlp_w_down", 0, 8192, 2048),
         ("odd_w_in", 0, 2048, 6144), ("odd_w_out", 0, 2048, 2048),
         ("mem_wk", 1, 2048, 512), ("mem_wv", 1, 2048, 512), ("mem_wq", 1, 2048, 512), ("mem_wo", 1, 512, 2048),
         ("mlp_w_up", 1, 2048, 8192), ("mlp_w_down", 1, 8192, 2048)]
WOFF = {}
WIDX = {}
_o = 0
for _i, (_n, _l, _K, _N) in enumerate(WMATS):
    WOFF[(_n, _l)] = (_o, _K, _N)
    WIDX[(_n, _l)] = _i
    _o += _K * _N // 8
WTOT8 = _o
CAT_OFF = 108 * 1024
NW = 4
GI = {"mix_pre": 0, "mix_post": 2, "mem_norm": 4, "mem_pre": 6, "mem_post": 8, "mlp_pre": 10, "mlp_post": 12}


class Ctx:
    SB_BYTES = 192 * 1024 - 256

    def __init__(self, nc, sb, ps, dr, wplan=None, dry=False):
        self.nc = nc
        self.s = Sched()
        self.s.dry = dry
        self.sb = sb
        self.ps = ps
        self.dr = dr
        self.top = 0
        self.s.qsems = {"sp": ["dsp%d" % i for i in range(12)], "pool": ["dpl%d" % i for i in range(6)],
                        "cc": ["dcc%d" % i for i in range(4)]}
        self.s.qnext = {"sp": 0, "pool": 0, "cc": 0}
        self.wrecd = set()
        self.banks = self.s.register(Alloc("ps", "psum", 0, 8, 2048,
                                           ps.rearrange("p (b n) -> p b n", b=8)))
        self.da = {}
        self.g = {}
        self.wplan = wplan
        self.wrec = []
        self.wi = 0
        self.wissued = 0
        self.rr = 0
        self.dumps = []

    def sem_names(self):
        n = list(ENGS)
        for q in self.s.qsems.values():
            n += q
        return n

    def alloc(self, name, ntiles, tile_shape, dt, at=None):
        esz = 4 if dt == F32 else 2
        n = _prod(tile_shape)
        tb = n * esz
        lo = self.top if at is None else at
        lo = (lo + 31) // 32 * 32
        hi = lo + ntiles * tb
        assert hi <= self.SB_BYTES, ("SBUF overflow", name, hi)
        if at is None:
            self.top = hi
        ap = self.sb[:, lo // 4:(hi + 3) // 4]
        if dt != F32:
            ap = ap.bitcast(dt)
        flat = ap[:, 0:ntiles * n]
        names = " ".join("d%d" % i for i in range(len(tile_shape)))
        kw = {"d%d" % i: v for i, v in enumerate(tile_shape)}
        ap = flat.rearrange("p (t %s) -> p t %s" % (names, names), t=ntiles, **kw)
        a = self.s.register(Alloc("sb", name, lo, ntiles, tb, ap))
        a.flat = flat
        return a

    def dten(self, name, ntiles=1):
        if name not in self.da:
            self.da[name] = self.s.register(Alloc("dram:" + name, name, 0, ntiles, 1, None))
        return self.da[name]

    def bank(self, i, j=None):
        return self.banks.t(i, j)

    def pb(self, i):
        return self.banks[i]

    def pb2(self, i, n):
        return self.ps[:, i * 512:i * 512 + n]

    def mm(self, out, lhsT, rhs, start, stop, reads, writes):
        return self.s.op("pe", lambda e: e.matmul(out, lhsT, rhs, start=start, stop=stop), reads, writes)

    def tr(self, out, in_, ident, reads, writes):
        return self.s.op("pe", lambda e: e.transpose(out, in_, ident), reads, writes)

    def act(self, out, in_, func, reads, writes, bias=None, scale=None, accum_out=None):
        kw = {}
        if bias is not None:
            kw["bias"] = bias
        if scale is not None:
            kw["scale"] = scale
        if accum_out is not None:
            kw["accum_out"] = accum_out
        return self.s.op("act", lambda e: e.activation(out=out, in_=in_, func=func, **kw), reads, writes)

    def copy(self, out, in_, reads, writes, eng=None):
        self.rr += 1
        if (self.rr % 2 and eng is None) or eng == "act":
            return self.act(out, in_, AF.Copy, reads, writes)
        return self.s.op("dve", lambda e: e.tensor_copy(out=out, in_=in_), reads, writes)

    def dma(self, q, out, in_, reads, writes):
        return self.s.op(q, lambda e: e.dma_start(out=out, in_=in_), reads, writes, dma=q)

    def dv(self, meth, kw, reads, writes, eng="dve"):
        return self.s.op(eng, lambda e: getattr(e, meth)(**kw), reads, writes)

    def pl(self, meth, kw, reads, writes):
        return self.dv(meth, kw, reads, writes, eng="pool")

    def wget(self, spec):
        if self.s.dry:
            self.wrec.append(spec)
            return self.g["wslots"][0]
        i = self.wi
        self.wi += 1
        assert self.wplan[i] == spec, (i, self.wplan[i], spec)
        while self.wissued < min(len(self.wplan), i + NW):
            self._wissue(self.wissued)
            self.wissued += 1
        return self.g["wslots"][i % NW]

    def _ensure_mat(self, key, phase=None, after=()):
        off, K, N = WOFF[key]
        sz = K * N // NCORES
        wl = self.dten("wl", len(WMATS))
        wf = self.dten("wf_%s%d" % key, 1)
        mi = WIDX[key]
        if phase in (None, "cast") and (key, "cast") not in self.wrecd:
            self.wrecd.add((key, "cast"))
            self.dma("pool", self.dr["wl"][off:off + sz].rearrange("(a b) -> a b", a=64),
                     self.dr["wshard"][off:off + sz].rearrange("(a b) -> a b", a=64), list(after), [wl.t(mi)])
        if phase in (None, "coll") and (key, "coll") not in self.wrecd:
            self.wrecd.add((key, "coll"))
            i_ap = self.dr["wl"][off:off + sz].rearrange("(k n) -> k n", n=min(K, 2048) * 2)
            o_ap = self.dr["wf_%s%d" % key]
            grp = [list(range(NCORES))]
            self.s.op("pool", lambda e: e.collective_compute("AllGather", ALU.bypass, replica_groups=grp,
                                                             ins=[i_ap.opt()], outs=[o_ap.opt()]),
                      [wl.t(mi)], [wf.t()], dma="cc")

    def ensure_mats(self, keys, after=()):
        if self.s.dry:
            return
        used = set((p[0], p[1]) for p in self.wplan)
        for key in keys:
            if key in used:
                self._ensure_mat(key, after=after)

    def _wissue(self, i):
        for j in range(i, min(len(self.wplan), i + WLOOK)):
            self._ensure_mat((self.wplan[j][0], self.wplan[j][1]))
        name, layer, k0, nk, c0, ncols = self.wplan[i]
        slot = self.g["wslots"][i % NW]
        wf = self.dten("wf_%s%d" % (name, layer), 1)
        N = WOFF[(name, layer)][2]
        j = (k0 // (nk * 128)) * (N // WCOLS) + c0 // WCOLS
        src = self.dr["wf_%s%d" % (name, layer)][j * 128:(j + 1) * 128, :]
        self.dma("sp", slot.flat[:, 0:nk * WCOLS], src, [wf.t()], [slot.t()])

    def dump(self, name, ap, shape, dt, reads):
        if self.s.dry:
            return
        t = self.nc.dram_tensor("dbg_" + name, list(shape), dt, kind="ExternalOutput").ap()
        self.dumps.append("dbg_" + name)
        tok = self.dma("sp", t, ap, reads, [self.dten("dbg_" + name).t()])
        self.g.setdefault("final", []).append(tok)


def gain(c, key, layer, ch):
    gi = GI[key] + layer
    return c.g["gains"][0][:, gi * 16 + ch:gi * 16 + ch + 1]


def load_consts(c):
    g = c.g
    g["ident"] = c.alloc("ident", 1, [128], F32)
    c.dma("sp", g["ident"][0], c.dr["ident"], [], [g["ident"].t()])
    g["ones"] = c.alloc("ones", 1, [128], BF16)
    c.dv('memset', dict(ap=g["ones"][0], constant=1.0), [], [g["ones"].t()])
    g["perm"] = c.alloc("perm", 1, [32], BF16)
    c.dma("pool", g["perm"][0][0:32, :], c.dr["perm"], [], [g["perm"].t()])
    g["gains"] = c.alloc("gains", 1, [14 * 16], F32)
    gt = g["gains"]
    c.dma("sp", gt[0], c.dr["gains"], [], [gt.t()])
    c.dv('tensor_scalar', dict(out=gt[0], in0=gt[0], scalar1=float(math.sqrt(D)), scalar2=None, op0=ALU.mult),
          [gt.t()], [gt.t()])
    g["sq"] = c.alloc("sq", 3, [512], BF16)
    g["rstd"] = c.alloc("rstd", 2, [512], F32)
    g["wslots"] = [c.alloc("w%d" % i, 1, [16, WCOLS], BF16) for i in range(NW)]
    g["sqi"] = 0
    g["rsi"] = 0
    g["ssb"] = 0
    g["base"] = c.top


def rms_stats(c, xap, xres, ntok, n_ch=16):
    g = c.g
    bk = 4 + (g["ssb"] % 2)
    g["ssb"] += 1
    for ch in range(n_ch):
        k = g["sqi"] % 3
        g["sqi"] += 1
        sq = g["sq"]
        c.act(sq[k][:, 0:ntok], xap(ch), AF.Square, [xres(ch)], [sq.t(k)])
        c.mm(c.pb(bk)[:, 0:ntok], g["ones"][0], sq[k][:, 0:ntok], ch == 0, ch == n_ch - 1,
             [sq.t(k), g["ones"].t()], [c.bank(bk)])
    r = g["rsi"] % 2
    g["rsi"] += 1
    rs = g["rstd"]
    c.act(rs[r][:, 0:ntok], c.pb(bk)[:, 0:ntok], AF.Sqrt, [c.bank(bk)], [rs.t(r)], bias=float(D * EPS), scale=1.0)
    c.dv('reciprocal', dict(out=rs[r][:, 0:ntok], in_=rs[r][:, 0:ntok]), [rs.t(r)], [rs.t(r)])
    return rs[r][:, 0:ntok], rs.t(r)


def rms_apply(c, xap, xres, key, layer, rstd, rres, outap, outres, n_ch=16):
    for ch in range(n_ch):
        gp = gain(c, key, layer, ch)
        c.dv('scalar_tensor_tensor', dict(out=outap(ch), in0=xap(ch), scalar=gp, in1=rstd,
                                                              op0=ALU.mult, op1=ALU.mult),
              [xres(ch), rres, c.g["gains"].t()], [outres(ch)])


def transpose_in(c, src_rows, ntile_rows, dst_ap, dst_res, xtok):
    g = c.g
    for e in range(ntile_rows):
        xb = e % 2
        c.dma("sp", xtok[xb], src_rows(e), [], [xtok.t(xb)])
        for g4 in range(4):
            bk = g4
            for k in range(4):
                ch = g4 * 4 + k
                c.tr(c.pb(bk)[:, k * 128:(k + 1) * 128], xtok[xb][:, ch * 128:(ch + 1) * 128], g["ident"][0],
                     [xtok.t(xb), g["ident"].t()], [c.bank(bk)])
            c.copy(dst_ap(g4, e), c.pb(bk).rearrange("p (k n) -> p k n", k=4), [c.bank(bk)], dst_res(g4, e))


def proj_fm(c, spec, nk, rhs, nblk, evac, bank0=0):
    w = c.wget(spec)
    wa = w.ap[:, 0]
    nj = spec[5] // 128
    for j in range(nj):
        base = bank0 + (c.g.setdefault("pjrot", 0) % 2) * nblk
        c.g["pjrot"] += 1
        for kc in range(nk):
            for b in range(nblk):
                r_ap, r_res = rhs(kc, b)
                n = r_ap.shape[-1]
                c.mm(c.pb(base + b)[:, 0:n], wa[:, kc, j * 128:(j + 1) * 128], r_ap, kc == 0, kc == nk - 1,
                     [w.t()] + r_res, [c.bank(base + b)])
        for b in range(nblk):
            evac(j, b, base + b)


def proj_tm(c, spec, nk, lhs, evac, bank0=6):
    w = c.wget(spec) if not isinstance(spec, Alloc) else spec
    wa = w.ap[:, 0]
    bk = bank0 + (c.g.setdefault("ptrot", 0) % 2)
    c.g["ptrot"] += 1
    ncols = WCOLS
    for kc in range(nk):
        l_ap, l_res = lhs(kc)
        c.mm(c.pb(bk)[:, 0:ncols], l_ap, wa[:, kc, 0:ncols], kc == 0, kc == nk - 1, [w.t()] + l_res, [c.bank(bk)])
    evac(bk)


def rope_fm(c, dst, dres, ntok, tab0):
    g = c.g
    rp = g["rope"]
    bk = 6 + (g.setdefault("rprot", 0) % 2)
    g["rprot"] += 1
    k = g["rprot"] % 2
    tmp = g["ropetmp"]
    c.mm(c.pb(bk)[0:32, 0:ntok], g["perm"][0][0:32, :], dst[0:32, :], True, True, [g["perm"].t()] + dres, [c.bank(bk)])
    c.dv('tensor_tensor', dict(out=tmp[2 * k][0:32, 0:ntok], in0=c.pb(bk)[0:32, 0:ntok],
                                    in1=rp[1][0:32, tab0:tab0 + ntok], op=ALU.mult),
          [c.bank(bk), rp.t(1)], [tmp.t(2 * k)])
    c.dv('tensor_tensor', dict(out=tmp[2 * k + 1][0:32, 0:ntok], in0=dst[0:32, :],
                                    in1=rp[0][0:32, tab0:tab0 + ntok], op=ALU.mult),
          dres + [rp.t(0)], [tmp.t(2 * k + 1)])
    c.dv('tensor_tensor', dict(out=dst[0:32, :], in0=tmp[2 * k][0:32, 0:ntok], in1=tmp[2 * k + 1][0:32, 0:ntok],
                                    op=ALU.add), [tmp.t(2 * k), tmp.t(2 * k + 1)], dres)


def l0_front(c):
    g = c.g
    c.top = g["base"]
    hn = c.alloc("hnT", 16 * 12, [128], BF16)
    g["hnT"] = hn
    mark = c.top
    xtok = c.alloc("xtok", 2, [2048], F32)
    xT = c.alloc("xT", 2 * 16, [512], F32)
    hT = c.dten("hT", 2)
    x_ext = c.dr["x_ext"]
    for b in range(3):
        buf = b % 2
        transpose_in(c, lambda e4: x_ext[(b * 4 + e4) * 128:(b * 4 + e4 + 1) * 128, :], 4,
                     lambda g4, e4: xT.ap[:, buf * 16 + g4 * 4:buf * 16 + g4 * 4 + 4, e4 * 128:(e4 + 1) * 128],
                     lambda g4, e4: [xT.t(buf * 16 + g4 * 4, buf * 16 + g4 * 4 + 4)], xtok)
        lo, hi = max(256, b * 512), min(1280, b * 512 + 512)
        c.dma("sp", c.dr["hT"][:, :, lo - 256:hi - 256], xT.ap[:, buf * 16:buf * 16 + 16, lo - b * 512:hi - b * 512],
              [xT.t(buf * 16, buf * 16 + 16)], [hT.t()])
        rstd, rres = rms_stats(c, lambda ch: xT[buf * 16 + ch], lambda ch: xT.t(buf * 16 + ch), 512)
        hflat = hn.flat.rearrange("p (c t) -> p c t", c=16)
        rms_apply(c, lambda ch: xT[buf * 16 + ch], lambda ch: xT.t(buf * 16 + ch), "mix_pre", 0, rstd, rres,
                  lambda ch: hflat[:, ch, b * 512:(b + 1) * 512], lambda ch: hn.t(ch * 12 + b * 4, ch * 12 + b * 4 + 4))
    c.top = mark


def hn_rhs(hn, ntile_per_chunk, t0, n):
    fl = hn.flat.rearrange("p (c t) -> p c t", c=16)

    def f(kc):
        return fl[:, kc, t0:t0 + n], [hn.t(kc * ntile_per_chunk + t0 // 128, kc * ntile_per_chunk + (t0 + n + 127) // 128)]
    return f


def l0_inproj(c):
    g = c.g
    hn = g["hnT"]
    rp = c.alloc("rope", 2, [1280], F32)
    g["rope"] = rp
    c.dma("sp", rp.ap[0:32], c.dr["rope"], [], [rp.t()])
    g["ropetmp"] = c.alloc("ropetmp", 4, [512], F32)
    vA = c.alloc("vA", 12, [8, 130], BF16)
    vS = c.alloc("vS", 10, [2, 130], BF16)
    g["vA"], g["vS"] = vA, vS
    c.dv('memset', dict(ap=vA.ap[:, :, :, 128:130], constant=1.0), [], [vA.t()])
    c.dv('memset', dict(ap=vS.ap[:, :, :, 128:130], constant=1.0), [], [vS.t()])
    mark = c.top
    stg = c.alloc("stg", 8, [512], BF16)
    si = [0]

    def fm_group(wname, col0, nheads, dname, tok_blocks, rope_tab):
        dt_ = c.dten(dname, nheads)
        for t in range(nheads * 128 // WCOLS):
            spec = (wname, 0, 0, 16, col0 + t * WCOLS, WCOLS)

            def rhs(kc, b):
                t0, n = tok_blocks[b]
                return hn_rhs(hn, 12, t0, n)(kc)

            def evac(j, b, bk):
                h = t * (WCOLS // 128) + j
                t0, n = tok_blocks[b]
                k = si[0] % 8
                si[0] += 1
                c.copy(stg[k][:, 0:n], c.pb(bk)[:, 0:n], [c.bank(bk)], [stg.t(k)])
                if rope_tab is not None:
                    rope_fm(c, stg[k][:, 0:n], [stg.t(k)], n, rope_tab + t0 - tok_blocks[0][0])
                c.dma("sp", c.dr[dname][h, :, t0 - tok_blocks[0][0]:t0 - tok_blocks[0][0] + n], stg[k][:, 0:n],
                      [stg.t(k)], [dt_.t(h)])
            proj_fm(c, spec, 16, rhs, len(tok_blocks), evac)

    own = [(256, 512), (768, 512)]
    fm_group("even_w_in", 0, 8, "qaT", own, None)
    fm_group("even_w_in", 1024, 8, "kaT", [(0, 512), (512, 512), (1024, 512)], None)
    for t in range(1024 // WCOLS):
        spec = ("even_w_in", 0, 0, 16, 2048 + t * WCOLS, WCOLS)
        w = c.wget(spec)
        for e in range(12):
            def evac(bk, e=e, t=t):
                nh = WCOLS // 128
                c.copy(vA.ap[:, e, t * nh:(t + 1) * nh, 0:128], c.pb(bk)[:, 0:WCOLS].rearrange("p (h d) -> p h d", h=nh),
                       [c.bank(bk)], [vA.t(e)])
            proj_tm(c, w, 16, lambda kc, e=e: hn_rhs(hn, 12, e * 128, 128)(kc), evac)
    fm_group("even_w_in", 3072, 8, "qsT", own, 128)
    fm_group("even_w_in", 4096, 2, "ksT", [(128, 512), (640, 512), (1152, 256)], 0)
    spec = ("even_w_in", 0, 0, 16, 4352, WCOLS)
    w = c.wget(spec)
    for e in range(10):
        def evac(bk, e=e):
            c.copy(vS.ap[:, e, :, 0:128], c.pb(bk)[:, 0:256].rearrange("p (h d) -> p h d", h=2), [c.bank(bk)], [vS.t(e)])
        proj_tm(c, w, 16, lambda kc, e=e: hn_rhs(hn, 12, (e + 1) * 128, 128)(kc), evac)
    c.top = mark


def pipeline(n, stages):
    ns = len(stages)
    for tick in range(n + ns - 1):
        for s, f in enumerate(stages):
            j = tick - s
            if 0 <= j < n:
                f(j)


def tail_norm(c, acc_bk, ncol, extra_den, den, on, k):
    g = c.g
    a = c.pb(acc_bk)
    if extra_den is not None:
        c.dv('tensor_tensor', dict(out=den[k][:, 0:1], in0=a[:, ncol:ncol + 1], in1=extra_den, op=ALU.add),
             [c.bank(acc_bk), g["esink"].t()], [den.t(k)])
        c.dv('reciprocal', dict(out=den[k][:, 0:1], in_=den[k][:, 0:1]), [den.t(k)], [den.t(k)])
    else:
        c.dv('reciprocal', dict(out=den[k][:, 0:1], in_=a[:, ncol:ncol + 1]), [c.bank(acc_bk)], [den.t(k)])
    c.dv('tensor_scalar', dict(out=on[k], in0=a[:, 0:ncol], scalar1=den[k][:, 0:1], scalar2=None, op0=ALU.mult),
         [c.bank(acc_bk), den.t(k)], [on.t(k)])


def tail_tr(c, on, k, dst, dres, tb):
    g = c.g
    c.tr(c.pb(tb)[:, 0:128], on[k], g["ident"][0], [on.t(k), g["ident"].t()], [c.bank(tb)])
    c.copy(dst, c.pb(tb)[:, 0:128], [c.bank(tb)], dres)


def l0_attn(c):
    g = c.g
    vA, vS = g["vA"], g["vS"]

    cat = c.alloc("catT", 16 * 2, [512], BF16, at=g["base"] + CAT_OFF)
    g["catT"] = cat
    catf = cat.flat.rearrange("p (c t) -> p c t", c=16)
    mark = c.top
    c.top = g["base"]
    qb = c.alloc("qb", 2, [1024], BF16)
    kb = c.alloc("kb", 2, [1536], BF16)
    kbs = c.alloc("kbs", 2, [1280], BF16)
    bias = c.alloc("nabias", 3, [768], F32)
    sc = c.alloc("sc", 3, [768], F32)
    pt = c.alloc("pt", 3, [768], BF16)
    den = c.alloc("den", 3, [8], F32)
    on = c.alloc("on", 3, [128], F32)
    swm = c.alloc("swm", 8, [384], F32)
    esink = c.alloc("esink", 1, [8], F32)
    g["esink"] = esink
    c.dma("sp", swm.ap, c.dr["sw_mask"], [], [swm.t()])
    c.dma("sp", esink[0], c.dr["sinks"], [], [esink.t()])
    c.act(esink[0], esink[0], AF.Exp, [esink.t()], [esink.t()])
    qda, kda = c.dten("qaT", 8), c.dten("kaT", 8)
    qds, kds = c.dten("qsT", 8), c.dten("ksT", 2)
    for gq in range(2):
        c.dma("sp", kbs[gq], c.dr["ksT"][gq], [kds.t(gq)], [kbs.t(gq)])
    items = [("na", h, a) for h in range(8) for a in range(8)] + [("sw", h, a) for h in range(8) for a in range(8)]

    def stA(i):
        kind, h, a = items[i]
        k, k2 = i % 3, i % 2
        hb = (i // 8) % 2
        sb0 = 2 * k2
        if kind == "na":
            if a == 0:
                c.dma("sp", qb[hb], c.dr["qaT"][h], [qda.t(h)], [qb.t(hb)])
                c.dma("sp", kb[hb], c.dr["kaT"][h], [kda.t(h)], [kb.t(hb)])
            e0 = a if a < 6 else a - 1
            c.dma("sp", bias[k], c.dr["na_bias"][a, h], [], [bias.t(k)])
            for j in range(6):
                c.mm(c.pb2(sb0, 1024)[:, j * 128:(j + 1) * 128], kb[hb][:, (e0 + j) * 128:(e0 + j + 1) * 128],
                     qb[hb][:, a * 128:(a + 1) * 128], True, True, [kb.t(hb), qb.t(hb)], [c.bank(sb0 + j // 4)])
            c.dv('scalar_tensor_tensor', dict(out=sc[k], in0=c.pb2(sb0, 768), scalar=float(SCALE), in1=bias[k],
                                              op0=ALU.mult, op1=ALU.add), [c.bank(sb0, sb0 + 2), bias.t(k)], [sc.t(k)])
            c.act(pt[k], sc[k], AF.Exp, [sc.t(k)], [pt.t(k)])
        else:
            gq = h // 4
            if a == 0:
                c.dma("sp", qb[hb], c.dr["qsT"][h], [qds.t(h)], [qb.t(hb)])
            for j in range(3):
                c.mm(c.pb(sb0)[:, j * 128:(j + 1) * 128], kbs[gq][:, (a + j) * 128:(a + j + 1) * 128],
                     qb[hb][:, a * 128:(a + 1) * 128], True, True, [kbs.t(gq), qb.t(hb)], [c.bank(sb0)])
            c.dv('scalar_tensor_tensor', dict(out=sc[k][:, 0:384], in0=c.pb(sb0)[:, 0:384], scalar=float(SCALE),
                                              in1=swm[a], op0=ALU.mult, op1=ALU.add), [c.bank(sb0), swm.t(a)], [sc.t(k)])
            c.act(pt[k][:, 0:384], sc[k][:, 0:384], AF.Exp, [sc.t(k)], [pt.t(k)])

    def stB(i):
        kind, h, a = items[i]
        k, ab = i % 3, 4 + i % 2
        if kind == "na":
            e0 = a if a < 6 else a - 1
            for j in range(6):
                c.mm(c.pb(ab)[:, 0:129], pt[k][:, j * 128:(j + 1) * 128], vA.ap[:, e0 + j, h, 0:129], j == 0, j == 5,
                     [pt.t(k), vA.t(e0 + j)], [c.bank(ab)])
            tail_norm(c, ab, 128, None, den, on, k)
        else:
            gq = h // 4
            for j in range(3):
                c.mm(c.pb(ab)[:, 0:129], pt[k][:, j * 128:(j + 1) * 128], vS.ap[:, a + j, gq, 0:129], j == 0, j == 2,
                     [pt.t(k), vS.t(a + j)], [c.bank(ab)])
            tail_norm(c, ab, 128, esink[0][:, h:h + 1], den, on, k)

    def stC(i):
        kind, h, a = items[i]
        ch = h if kind == "na" else 8 + h
        tail_tr(c, on, i % 3, catf[:, ch, a * 128:(a + 1) * 128], [cat.t(ch * 2 + a // 4)], 6 + i % 2)

    pipeline(len(items), [stA, stB, stC])
    c.top = mark


def proj_Y(c, wname, layer, nk, src, Y):
    yf = Y.flat.rearrange("p (c t) -> p c t", c=16)
    for t in range(2048 // WCOLS):
        spec = (wname, layer, 0, nk, t * WCOLS, WCOLS)

        def evac(j, b, bk):
            n = t * (WCOLS // 128) + j
            c.copy(yf[:, n, b * 512:(b + 1) * 512], c.pb(bk), [c.bank(bk)], [Y.t(n * 2 + b)])
        proj_fm(c, spec, nk, src, 2, evac)


def post_pre(c, Y, layer, post_key, pre_key, pre_layer, hn_out, final=False):
    g = c.g
    mark = c.top
    H = c.alloc("H", 16, [512], F32)
    hT = c.dten("hT", 2)
    tmp = c.alloc("pp_tmp", 4, [512], F32)
    if final:
        otok = c.alloc("otok", 1, [2048], F32)
    for b in range(2):
        rstd, rres = rms_stats(c, lambda ch: Y[ch * 2 + b], lambda ch: Y.t(ch * 2 + b), 512)
        c.dma("sp", H.ap, c.dr["hT"][:, :, b * 512:(b + 1) * 512], [hT.t(b)], [H.t()])
        for ch in range(16):
            k = ch % 4
            gp = gain(c, post_key, layer, ch)
            c.pl('tensor_tensor', dict(out=tmp[k], in0=Y[ch * 2 + b], in1=rstd, op=ALU.mult),
                 [Y.t(ch * 2 + b), rres], [tmp.t(k)])
            c.dv('scalar_tensor_tensor', dict(out=H[ch], in0=tmp[k], scalar=gp, in1=H[ch], op0=ALU.mult, op1=ALU.add),
                 [H.t(ch), tmp.t(k), g["gains"].t()], [H.t(ch)])
        if not final:
            c.dma("sp", c.dr["hT"][:, :, b * 512:(b + 1) * 512], H.ap, [H.t()], [hT.t(b)])
        if pre_key is not None:
            rstd2, rres2 = rms_stats(c, lambda ch: H[ch], lambda ch: H.t(ch), 512)
            rms_apply(c, lambda ch: H[ch], lambda ch: H.t(ch), pre_key, pre_layer, rstd2, rres2,
                      lambda ch: hn_out[ch * 2 + b], lambda ch: hn_out.t(ch * 2 + b))
        if final:
            od = c.dten("out", 8)
            for tt in range(4):
                ob = 0
                for g4 in range(4):
                    bk = g4
                    for k in range(4):
                        ch = g4 * 4 + k
                        c.tr(c.pb(bk)[:, k * 128:(k + 1) * 128], H[ch][:, tt * 128:(tt + 1) * 128], g["ident"][0],
                             [H.t(ch), g["ident"].t()], [c.bank(bk)])
                    c.copy(otok[ob][:, g4 * 512:(g4 + 1) * 512], c.pb(bk), [c.bank(bk)], [otok.t(ob)])
                row = (b * 4 + tt) * 128
                tok = c.dma("sp", c.dr["out"][row:row + 128, :], otok[ob], [otok.t(ob)], [od.t(b * 4 + tt)])
                g.setdefault("final", []).append(tok)
    c.top = mark


def mem_kv(c, layer):
    g = c.g
    kmT, vM = g["kmT"], g["vM"]
    c.dv('memset', dict(ap=vM.ap[:, :, :, 128:130], constant=1.0), [], [vM.t()])
    mark = c.top
    xtok = c.alloc("mxtok", 2, [2048], F32)
    mT = c.alloc("mT", 16, [256], F32)
    mn = c.alloc("mnT", 16, [256], BF16)
    mem = c.dr["mem"]
    transpose_in(c, lambda e: mem[e * 128:(e + 1) * 128, :], 2,
                 lambda g4, e: mT.ap[:, g4 * 4:g4 * 4 + 4, e * 128:(e + 1) * 128],
                 lambda g4, e: [mT.t(g4 * 4, g4 * 4 + 4)], xtok)
    rstd, rres = rms_stats(c, lambda ch: mT[ch], lambda ch: mT.t(ch), 256)
    rms_apply(c, lambda ch: mT[ch], lambda ch: mT.t(ch), "mem_norm", layer, rstd, rres,
              lambda ch: mn[ch], lambda ch: mn.t(ch))
    for t in range(512 // WCOLS):
        spec = ("mem_wk", layer, 0, 16, t * WCOLS, WCOLS)

        def evac(j, b, bk):
            h = t * (WCOLS // 128) + j
            c.copy(kmT[h], c.pb(bk)[:, 0:256], [c.bank(bk)], [kmT.t(h)])
        proj_fm(c, spec, 16, lambda kc, b: (mn[kc], [mn.t(kc)]), 1, evac)
    for t in range(512 // WCOLS):
        spec = ("mem_wv", layer, 0, 16, t * WCOLS, WCOLS)
        w = c.wget(spec)
        for e in range(2):
            def evac(bk, e=e, t=t):
                nh = WCOLS // 128
                c.copy(vM.ap[:, e, t * nh:(t + 1) * nh, 0:128], c.pb(bk)[:, 0:WCOLS].rearrange("p (h d) -> p h d", h=nh),
                       [c.bank(bk)], [vM.t(e)])
            proj_tm(c, w, 16, lambda kc, e=e: (mn[kc][:, e * 128:(e + 1) * 128], [mn.t(kc)]), evac)
    c.top = mark


def mem_attn(c, layer, hn, Y):
    g = c.g
    mark = c.top
    kmT, vM = g["kmT"], g["vM"]
    qm = c.alloc("qmT", 4 * 2, [512], BF16)
    oc = c.alloc("ocT", 4 * 2, [512], BF16)
    pt = c.alloc("mpt", 2, [2, 512], BF16)
    den = c.alloc("mden", 2, [8], F32)
    on = c.alloc("mon", 2, [128], F32)
    for t in range(512 // WCOLS):
        spec = ("mem_wq", layer, 0, 16, t * WCOLS, WCOLS)

        def evac(j, b, bk):
            h = t * (WCOLS // 128) + j
            c.copy(qm[h * 2 + b], c.pb(bk), [c.bank(bk)], [qm.t(h * 2 + b)])
        proj_fm(c, spec, 16, lambda kc, b: (hn[kc * 2 + b], [hn.t(kc * 2 + b)]), 2, evac)
    den = c.alloc("mden8", 8, [8], F32)
    on = c.alloc("mon8", 8, [128], F32)
    items = [(h, b) for h in range(4) for b in range(2)]

    def stA(i):
        h, b = items[i]
        k = i % 2
        for kt in range(2):
            c.mm(c.pb(2 * k + kt), kmT[h][:, kt * 128:(kt + 1) * 128], qm[h * 2 + b], True, True,
                 [kmT.t(h), qm.t(h * 2 + b)], [c.bank(2 * k + kt)])
        c.act(pt[k].rearrange("p a n -> p (a n)"), c.pb2(2 * k, 1024), AF.Exp, [c.bank(2 * k, 2 * k + 2)], [pt.t(k)],
              scale=float(SCALE))

    def stB(i):
        h, b = items[i]
        k = i % 2
        for qt in range(4):
            ab = 4 + (qt % 2)
            for kt in range(2):
                c.mm(c.pb(ab)[:, 0:129], pt[k][:, kt, qt * 128:(qt + 1) * 128], vM.ap[:, kt, h, 0:129], kt == 0, kt == 1,
                     [pt.t(k), vM.t(kt)], [c.bank(ab)])
            tail_norm(c, ab, 128, None, den, on, k * 4 + qt)

    def stC(i):
        h, b = items[i]
        k = i % 2
        for qt in range(4):
            tail_tr(c, on, k * 4 + qt, oc[h * 2 + b][:, qt * 128:(qt + 1) * 128], [oc.t(h * 2 + b)], 6 + qt % 2)

    pipeline(len(items), [stA, stB, stC])
    proj_Y(c, "mem_wo", layer, 4, lambda kc, b: (oc[kc * 2 + b], [oc.t(kc * 2 + b)]), Y)
    c.top = mark


def mlp(c, layer, hn, Y):
    g = c.g
    mark = c.top
    if layer == 0:
        c.ensure_mats([(m[0], m[1]) for m in WMATS[8:]])

    uT = c.alloc("uT", 16 * 2, [512], BF16)
    rl = c.alloc("relu", 2, [512], F32)
    yf = Y.flat.rearrange("p (c t) -> p c t", c=16)
    ri = [0]
    for qd in range(4):
        for t in range(2048 // WCOLS):
            spec = ("mlp_w_up", layer, 0, 16, qd * 2048 + t * WCOLS, WCOLS)

            def evac(j, b, bk):
                jj = t * (WCOLS // 128) + j
                k = ri[0] % 2
                ri[0] += 1
                c.act(rl[k], c.pb(bk), AF.Relu, [c.bank(bk)], [rl.t(k)])
                c.dv('tensor_tensor', dict(out=uT[jj * 2 + b], in0=rl[k], in1=rl[k], op=ALU.mult), [rl.t(k)],
                      [uT.t(jj * 2 + b)])
            proj_fm(c, spec, 16, lambda kc, b: (hn[kc * 2 + b], [hn.t(kc * 2 + b)]), 2, evac)
        for t in range(2048 // WCOLS):
            spec = ("mlp_w_down", layer, qd * 2048, 16, t * WCOLS, WCOLS)

            def evac(j, b, bk):
                n = t * (WCOLS // 128) + j
                dst = yf[:, n, b * 512:(b + 1) * 512]
                if qd == 0:
                    c.copy(dst, c.pb(bk), [c.bank(bk)], [Y.t(n * 2 + b)])
                else:
                    c.dv('tensor_tensor', dict(out=dst, in0=dst, in1=c.pb(bk), op=ALU.add),
                          [c.bank(bk), Y.t(n * 2 + b)], [Y.t(n * 2 + b)])
            proj_fm(c, spec, 16, lambda kc, b: (uT[kc * 2 + b], [uT.t(kc * 2 + b)]), 2, evac)
    c.top = mark


def layer_tail(c, layer, mix_src, mix_w, final):
    g = c.g
    c.top = g["base"]
    g["kmT"] = c.alloc("kmT", 4, [256], BF16)
    g["vM"] = c.alloc("vM", 2, [4, 130], BF16)
    Y = c.alloc("Y", 16 * 2, [512], F32)
    hn = c.alloc("hn", 16 * 2, [512], BF16)
    g["hn"] = hn
    base2 = c.top
    assert base2 <= g["base"] + CAT_OFF
    src_alloc = mix_src(c)
    proj_Y(c, mix_w, 0, 16, lambda kc, b: (src_alloc[kc * 2 + b], [src_alloc.t(kc * 2 + b)]), Y)
    c.top = base2
    mem_kv(c, layer)
    post_pre(c, Y, layer, "mix_post", "mem_pre", layer, hn)
    mem_attn(c, layer, hn, Y)
    post_pre(c, Y, layer, "mem_post", "mlp_pre", layer, hn)
    mlp(c, layer, hn, Y)
    if final:
        post_pre(c, Y, layer, "mlp_post", None, None, None, final=True)
    else:
        post_pre(c, Y, layer, "mlp_post", "mix_pre", layer + 1, hn)


LAMBDA_INIT1 = 0.8 - 0.6 * math.exp(-0.3 * 1)


def l1_inproj(c, gather=False):
    g = c.g
    hn = g["hn"]
    c.top = g["base"]
    rp = c.alloc("rope", 2, [1280], F32)
    g["rope"] = rp
    c.dma("sp", rp.ap[0:32], c.dr["rope"], [], [rp.t()])
    g["ropetmp"] = c.alloc("ropetmp", 4, [512], F32)
    stg = c.alloc("stg1", 4, [512], BF16)
    vst = c.alloc("vst", 3, [258], BF16)
    c.dv('memset', dict(ap=vst.ap[:, :, 256:258], constant=1.0), [], [vst.t()])
    qT = c.alloc("qT1", 16 * 2, [512], BF16, at=g["base"] + CAT_OFF)
    g["qT1"] = qT
    kd = c.dten("kloc", 16)
    vd = c.dten("vloc", 8)
    si = [0]
    src = lambda kc, b: (hn[kc * 2 + b], [hn.t(kc * 2 + b)])
    for t in range(2048 // WCOLS):
        spec = ("odd_w_in", 0, 0, 16, 2048 + t * WCOLS, WCOLS)

        def evac(j, b, bk):
            hp = t * (WCOLS // 128) + j
            k = si[0] % 4
            si[0] += 1
            c.copy(stg[k], c.pb(bk), [c.bank(bk)], [stg.t(k)])
            rope_fm(c, stg[k], [stg.t(k)], 512, 128 + b * 512)
            c.dma("sp", c.dr["kloc"][hp * 128:(hp + 1) * 128, b * 512:(b + 1) * 512], stg[k], [stg.t(k)], [kd.t(hp)])
        proj_fm(c, spec, 16, src, 2, evac)
    vi = [0]
    for h in range(8):
        spec = ("odd_w_in", 0, 0, 16, 4096 + h * 256, 256)
        w = c.wget(spec)
        for e in range(8):
            def evac(bk, e=e, h=h):
                k = vi[0] % 3
                vi[0] += 1
                c.copy(vst[k][:, 0:256], c.pb(bk)[:, 0:256], [c.bank(bk)], [vst.t(k)])
                c.dma("sp", c.dr["vloc"][e * 128:(e + 1) * 128, h * 258:(h + 1) * 258], vst[k], [vst.t(k)], [vd.t(e)])
            proj_tm(c, w, 16, lambda kc, e=e: (hn[kc * 2 + e // 4][:, (e % 4) * 128:(e % 4 + 1) * 128],
                                               [hn.t(kc * 2 + e // 4)]), evac)
    if gather:
        l1_gather(c, "k")
        l1_gather(c, "v")
    for t in range(2048 // WCOLS):
        spec = ("odd_w_in", 0, 0, 16, t * WCOLS, WCOLS)

        def evac(j, b, bk):
            hp = t * (WCOLS // 128) + j
            c.copy(qT[hp * 2 + b], c.pb(bk), [c.bank(bk)], [qT.t(hp * 2 + b)])
            rope_fm(c, qT[hp * 2 + b], [qT.t(hp * 2 + b)], 512, 128 + b * 512)
        proj_fm(c, spec, 16, src, 2, evac)


def l1_gather(c, which):
    grp = [list(range(NCORES))]
    loc, full, nt = ("kloc", "kfull", 16) if which == "k" else ("vloc", "vfull", 8)
    dl, dfu = c.dten(loc, nt), c.dten(full, 1)
    i_ap, o_ap = c.dr[loc], c.dr[full]
    c.s.op("pool", lambda e: e.collective_compute("AllGather", ALU.bypass, replica_groups=grp, ins=[i_ap.opt()],
                                                  outs=[o_ap.opt()]), [dl.t()], [dfu.t()], dma="cc")


def l1_attn(c):
    g = c.g
    qT = g["qT1"]
    c.top = g["base"]
    kq = c.alloc("kq", 5, [2, 2048], BF16)
    vq = c.alloc("vq", 5, [16, 258], BF16)
    pt = c.alloc("pt1", 3, [512], BF16)
    lv = c.alloc("lamv", 4, [128], F32)
    lt = c.alloc("lamt", 8, [2], F32)
    sg = c.alloc("sgs", 1, [256], F32)
    rr = c.alloc("rr", 4, [4], F32)
    tmp = c.alloc("otmp", 2, [256], F32)
    ob = c.alloc("obuf", 2, [256], F32)
    on = c.alloc("onrm", 4, [256], F32)
    jk = c.alloc("junk", 1, [256], F32)
    ost = c.alloc("ost", 2, [2, 256], BF16)
    kf, vf = c.dten("kfull", 1), c.dten("vfull", 1)
    cd = c.dten("cat1", 16)
    c.dma("sp", lv.ap, c.dr["lamvec"], [], [lv.t()])
    c.dma("sp", sg[0], c.dr["subln"], [], [sg.t()])
    c.dv('tensor_scalar', dict(out=sg[0], in0=sg[0], scalar1=float((1.0 - LAMBDA_INIT1) * 16.0), scalar2=None,
                               op0=ALU.mult), [sg.t()], [sg.t()])
    for i in range(2):
        c.dv('tensor_tensor', dict(out=lv[2 * i], in0=lv[2 * i], in1=lv[2 * i + 1], op=ALU.mult),
             [lv.t(2 * i, 2 * i + 2)], [lv.t(2 * i)])
        c.dv('reduce_sum', dict(out=lt[i][:, 0:1], in_=lv[2 * i], axis=AX.X), [lv.t(2 * i)], [lt.t(i)])
        c.act(lt[i][:, 0:1], lt[i][:, 0:1], AF.Exp, [lt.t(i)], [lt.t(i)])
    c.dv('tensor_tensor', dict(out=lt[2][:, 0:1], in0=lt[1][:, 0:1], in1=lt[0][:, 0:1], op=ALU.subtract),
         [lt.t(0, 2)], [lt.t(2)])
    c.dv('tensor_scalar', dict(out=lt[2][:, 0:1], in0=lt[2][:, 0:1], scalar1=float(-LAMBDA_INIT1), scalar2=None,
                               op0=ALU.add), [lt.t(2)], [lt.t(2)])
    nlam = lt[2][:, 0:1]

    kfull = c.dr["kfull"].rearrange("(r hp d) t -> d hp r t", r=NCORES, hp=16)
    vfull = c.dr["vfull"].rearrange("(kt p) c -> p kt c", p=128)

    def load_quarter(n):
        h, i = divmod(n, 4)
        buf = n % 5
        for t in range(2):
            c.dma("sp", kq[buf][:, t].rearrange("p (r t) -> p r t", r=2), kfull[:, 2 * h + t, 2 * i:2 * i + 2, :],
                  [kf.t()], [kq.t(buf)])
        c.dma("sp", vq[buf], vfull[:, i * 16:(i + 1) * 16, h * 258:(h + 1) * 258], [vf.t()], [vq.t(buf)])

    accs = c.alloc("accs", 8, [258], F32)
    assert c.top <= g["base"] + CAT_OFF, c.top - g["base"]
    for n in range(5):
        load_quarter(n)
    steps = [(h, qb, kt) for h in range(8) for qb in range(4) for kt in range(64)]
    pending = []

    def stA(s):
        h, qb, kt = steps[s]
        n = h * 4 + kt // 16
        buf = n % 5
        sbk, k3 = s % 2, s % 3
        for t in range(2):
            qa = qT[(2 * h + t) * 2 + qb // 2][:, (qb % 2) * 256:(qb % 2) * 256 + 256]
            c.mm(c.pb(sbk)[:, t * 256:(t + 1) * 256], kq[buf][:, t, (kt % 16) * 128:(kt % 16 + 1) * 128], qa,
                 True, True, [kq.t(buf), qT.t((2 * h + t) * 2 + qb // 2)], [c.bank(sbk)])
        c.act(pt[k3], c.pb(sbk), AF.Exp, [c.bank(sbk)], [pt.t(k3)], scale=float(SCALE))

    def stB(s):
        h, qb, kt = steps[s]
        n = h * 4 + kt // 16
        buf = n % 5
        k3 = s % 3
        for t in range(2):
            for qt in range(2):
                ab = 2 + t * 2 + qt
                c.mm(c.pb(ab)[:, 0:257], pt[k3][:, t * 256 + qt * 128:t * 256 + qt * 128 + 128],
                     vq[buf][:, kt % 16, 0:257], kt == 0, kt == 63, [pt.t(k3), vq.t(buf)], [c.bank(ab)])
        if qb == 3 and kt % 16 == 15 and n + 5 < 32:
            load_quarter(n + 5)
        if kt == 63:
            epilogue(h, qb, s)
        while pending and (pending[0][0] <= s or s == len(steps) - 1):
            pending.pop(0)[1]()

    def epilogue(h, qb, s_now):
        hq = h * 4 + qb
        ko = hq % 2
        for j in range(4):
            c.copy(accs[ko * 4 + j][:, 0:257], c.pb(2 + j)[:, 0:257], [c.bank(2 + j)], [accs.t(ko * 4 + j)], eng="dve")
        for qt in range(2):
            k = ko * 2 + qt
            a0, a1 = accs[ko * 4 + qt], accs[ko * 4 + 2 + qt]
            r0, r1 = accs.t(ko * 4 + qt), accs.t(ko * 4 + 2 + qt)
            c.dv('reciprocal', dict(out=rr[k][:, 0:1], in_=a0[:, 256:257]), [r0], [rr.t(k)])
            c.dv('reciprocal', dict(out=rr[k][:, 1:2], in_=a1[:, 256:257]), [r1], [rr.t(k)])
            c.dv('tensor_tensor', dict(out=rr[k][:, 2:3], in0=rr[k][:, 1:2], in1=nlam, op=ALU.mult),
                 [rr.t(k), lt.t(2)], [rr.t(k)])
            c.dv('tensor_scalar', dict(out=tmp[qt], in0=a0[:, 0:256], scalar1=rr[k][:, 0:1], scalar2=None, op0=ALU.mult),
                 [r0, rr.t(k)], [tmp.t(qt)])
            c.dv('scalar_tensor_tensor', dict(out=ob[qt], in0=a1[:, 0:256], scalar=rr[k][:, 2:3], in1=tmp[qt],
                                              op0=ALU.mult, op1=ALU.add), [r1, rr.t(k), tmp.t(qt)], [ob.t(qt)])
            c.dv('tensor_tensor', dict(out=jk[0], in0=ob[qt], in1=ob[qt], op=ALU.mult), [ob.t(qt)], [jk.t()])
            c.dv('reduce_sum', dict(out=rr[k][:, 3:4], in_=jk[0], axis=AX.X), [jk.t()], [rr.t(k)])
            c.act(rr[k][:, 3:4], rr[k][:, 3:4], AF.Ln, [rr.t(k)], [rr.t(k)], bias=float(256 * EPS), scale=1.0)
            c.act(rr[k][:, 3:4], rr[k][:, 3:4], AF.Exp, [rr.t(k)], [rr.t(k)], scale=-0.5)
            c.dv('scalar_tensor_tensor', dict(out=on[k], in0=ob[qt], scalar=rr[k][:, 3:4], in1=sg[0], op0=ALU.mult,
                                              op1=ALU.mult), [ob.t(qt), rr.t(k), sg.t()], [on.t(k)])
        pending.append((s_now + 6, lambda: epilogue2(h, qb)))

    def epilogue2(h, qb):
        hq = h * 4 + qb
        ko = hq % 2
        for qt in range(2):
            k = ko * 2 + qt
            for dvc in range(2):
                tb = 6 + dvc
                c.tr(c.pb(tb)[:, 0:128], on[k][:, dvc * 128:(dvc + 1) * 128], g["ident"][0],
                     [on.t(k), g["ident"].t()], [c.bank(tb)])
                c.copy(ost[ko][:, dvc, qt * 128:(qt + 1) * 128], c.pb(tb)[:, 0:128], [c.bank(tb)], [ost.t(ko)], eng="dve")
        c.dma("sp", c.dr["cat1"][:, 2 * h:2 * h + 2, qb * 256:(qb + 1) * 256], ost[ko], [ost.t(ko)],
              [cd.t(2 * h, 2 * h + 2)])

    pipeline(len(steps), [stA, stB])


def l1_mix_src(c):
    g = c.g
    cat = c.alloc("catT1", 16 * 2, [512], BF16, at=g["base"] + CAT_OFF)
    cd = c.dten("cat1", 16)
    c.dma("sp", cat.flat.rearrange("p (c t) -> p c t", c=16), c.dr["cat1"], [cd.t()], [cat.t()])
    return cat


STOP = None
DEBUG = False


def program(c, mode):
    g = c.g
    load_consts(c)
    c.ensure_mats([(m[0], m[1]) for m in (WMATS[:8] if mode != "B" else WMATS[8:])])
    if mode in ("A", "fused"):
        l0_front(c)
        if STOP == "front":
            c.dump("hnT", g["hnT"].flat, [128, 16 * 1536], BF16, [g["hnT"].t()])
            return c.s.dry or c.s.all_tokens_wait("sp", list(c.s.dma_cnt.items()))
        l0_inproj(c)
        if STOP == "inproj":
            c.dump("vA", g["vA"].flat, [128, 12 * 8 * 130], BF16, [g["vA"].t()])
            c.dump("vS", g["vS"].flat, [128, 10 * 2 * 130], BF16, [g["vS"].t()])
            return c.s.dry or c.s.all_tokens_wait("sp", list(c.s.dma_cnt.items()))
        l0_attn(c)
        if STOP == "attn":
            c.dump("catT", g["catT"].flat, [128, 16 * 1024], BF16, [g["catT"].t()])
            return c.s.dry or c.s.all_tokens_wait("sp", list(c.s.dma_cnt.items()))
        layer_tail(c, 0, lambda cc: cc.g["catT"], "even_w_out", final=False)
        if STOP == "l0":
            return c.s.dry or c.s.all_tokens_wait("sp", list(c.s.dma_cnt.items()))
        l1_inproj(c, gather=(mode == "fused"))
        if mode == "A":
            qT = g["qT1"]
            c.dma("sp", c.dr["qT1_d"], qT.ap, [qT.t()], [c.dten("qT1_d").t()])
    if mode in ("B", "fused"):
        if mode == "B":
            hT = c.dten("hT", 2)
            c.dma("sp", c.dr["hT"], c.dr["hT_in"], [], [hT.t()])
            qT = c.alloc("qT1", 16 * 2, [512], BF16, at=g["base"] + CAT_OFF)
            g["qT1"] = qT
            c.dma("sp", qT.ap, c.dr["qT1_d"], [], [qT.t()])
        l1_attn(c)
        layer_tail(c, 1, l1_mix_src, "odd_w_out", final=True)
    if not c.s.dry:
        c.s.all_tokens_wait("sp", list(c.s.dma_cnt.items()))


def build(mode, debug=None):
    from contextlib import ExitStack
    nc = bass.Bass("TRN2", target_bir_lowering=False)
    dr = {}

    def ten(name, shape, dt=F32, kind="ExternalInput"):
        if DEBUG and kind == "Internal" and name in ("qaT", "kaT", "qsT", "ksT", "cat1", "hT"):
            kind = "ExternalOutput"
        dr[name] = nc.dram_tensor(name, list(shape), dt, kind=kind).ap()

    for nm, shp in (("ident", [128, 128]), ("perm", [32, 32]), ("gains", [128, 224]), ("rope", [32, 2, 1280]),
                    ("mem", [256, 2048]), ("wshard", [WTOT8])):
        ten(nm, shp)
    ten("wl", [WTOT8], BF16, "Internal")
    for (wn, wlayer, wK, wN) in WMATS:
        ten("wf_%s%d" % (wn, wlayer), [wK * wN // (min(wK, 2048) * 2), min(wK, 2048) * 2], BF16, "Internal")
    A, B = mode in ("A", "fused"), mode in ("B", "fused")
    if A:
        for nm, shp in (("x_ext", [1536, 2048]), ("na_bias", [8, 8, 128, 768]), ("sw_mask", [128, 8, 384]),
                        ("sinks", [128, 8])):
            ten(nm, shp)
        for nm, shp in (("qaT", [8, 128, 1024]), ("kaT", [8, 128, 1536]), ("qsT", [8, 128, 1024]), ("ksT", [2, 128, 1280])):
            ten(nm, shp, BF16, "Internal")
    if B:
        for nm, shp in (("lamvec", [128, 4, 128]), ("subln", [128, 256])):
            ten(nm, shp)
        ten("cat1", [128, 16, 1024], BF16, "Internal")
        ten("out", [1024, 2048], F32, "ExternalOutput")
    ext_o = "ExternalOutput" if mode == "A" else "Internal"
    ext_i = "ExternalInput" if mode == "B" else "Internal"
    ten("hT", [128, 16, 1024], F32, ext_o)
    if mode == "B":
        ten("hT_in", [128, 16, 1024], F32, "ExternalInput")
    if A:
        ten("kloc", [2048, 1024], BF16, ext_o)
        ten("vloc", [1024, 2064], BF16, ext_o)
    if mode != "fused":
        ten("qT1_d", [128, 32, 512], BF16, "ExternalOutput" if mode == "A" else "ExternalInput")
    if B:
        ten("kfull", [NCORES * 2048, 1024], BF16, ext_i)
        ten("vfull", [NCORES * 1024, 2064], BF16, ext_i)
    with ExitStack() as st:
        sb = st.enter_context(nc.sbuf_tensor("arena", [128, Ctx.SB_BYTES // 4], F32))
        ps = st.enter_context(nc.psum_tensor("psum", [128, 4096], F32))
        cdry = Ctx(nc, sb, ps, dr, dry=True)
        program(cdry, mode)
        c = Ctx(nc, sb, ps, dr, wplan=cdry.wrec)
        c.debug = debug
        program(c, mode)
        sems = {n: st.enter_context(nc.semaphore(n)) for n in c.sem_names()}
        block = st.enter_context(nc.Block())
        c.s.emit(nc, sems, block)
    return nc, c


def _bf16():
    import ml_dtypes
    return ml_dtypes.bfloat16


def _host_consts(inputs):
    f32 = np.float32
    gl = []
    for k in ("mix_pre_g", "mix_post_g", "mem_norm_g", "mem_pre_g", "mem_post_g", "mlp_pre_g", "mlp_post_g"):
        for l in range(2):
            gl.append(np.asarray(inputs[k][l], f32).reshape(16, 128).T)
    gains = np.ascontiguousarray(np.stack(gl, axis=1).reshape(128, 224))
    perm = np.zeros((32, 32), f32)
    for m in range(32):
        perm[(m + 16) % 32, m] = 1.0
    com = {"ident": np.eye(128, dtype=f32), "perm": perm, "gains": gains,
           "mem": np.ascontiguousarray(inputs["mem"][0], f32)}
    return com


def _tiled(inputs, n, l, K, N):
    w = np.asarray(inputs[n][l], np.float32)
    nk = min(K, 2048) // 128
    wt = w.reshape(K // (nk * 128), nk, 128, N // WCOLS, WCOLS).transpose(0, 3, 2, 1, 4)
    return np.ascontiguousarray(wt).reshape(NCORES, -1)


def _wshards(inputs):
    tl = [_tiled(inputs, n, l, K, N) for (n, l, K, N) in WMATS]
    return [np.concatenate([t[r] for t in tl]) for r in range(NCORES)]


def _rope_table(start):
    f32 = np.float32
    inv = (1.0 / (f32(500000.0) ** (np.arange(0, 32, 2, dtype=f32) / f32(32)))).astype(f32)
    pos = (start - 128 + np.arange(1280)).astype(f32)
    ang = (pos[None, :] * inv[:, None]).astype(f32)
    cs, sn = np.cos(ang).astype(f32), np.sin(ang).astype(f32)
    tab = np.zeros((32, 2, 1280), f32)
    tab[0:16, 0], tab[16:32, 0] = cs, cs
    tab[0:16, 1], tab[16:32, 1] = -sn, sn
    return tab


def _na_bias(rpb, core):
    out = np.full((8, 8, 128, 6, 128), NEG, np.float32)
    t = np.arange(128)
    for a in range(8):
        G = 8 * core + a
        e0 = a if a < 6 else a - 1
        qi = 2 * G + t // 64
        qj = t % 64
        rs = np.clip(qi - 4, 0, 120)
        cs = np.clip(qj - 8, 0, 48)
        for j in range(6):
            Gk = 8 * core - 2 + e0 + j
            if Gk < 0 or Gk >= 64:
                continue
            kr = 2 * Gk + t // 64
            kc = t % 64
            valid = ((kr[:, None] >= rs[None, :]) & (kr[:, None] < rs[None, :] + 8) &
                     (kc[:, None] >= cs[None, :]) & (kc[:, None] < cs[None, :] + 16))
            ri = np.clip(kr[:, None] - qi[None, :] + 7, 0, 14)
            ci = np.clip(kc[:, None] - qj[None, :] + 15, 0, 30)
            vals = rpb[:, ri, ci]
            out[a, :, :, j, :] = np.where(valid[None], vals, np.float32(NEG))
    return out.reshape(8, 8, 128, 768)


def _sw_mask(core):
    m = np.zeros((128, 8, 3, 128), np.float32)
    kk = np.arange(128)[:, None]
    qq = np.arange(128)[None, :]
    for a in range(8):
        G = 8 * core + a
        m[:, a, 0, :] = np.where((qq <= kk) & (G - 1 >= 0), 0.0, NEG)
        m[:, a, 2, :] = np.where((kk <= qq) & (G + 1 < 64), 0.0, NEG)
    return m.reshape(128, 8, 384)


_PROGS = {}


def _get(mode):
    if mode not in _PROGS:
        _PROGS[mode] = build(mode)[0]
    return _PROGS[mode]


def _maps_A(inputs, com):
    f32 = np.float32
    x = np.asarray(inputs["x"][0], f32)
    xp = np.zeros((S + 256 + 256 + 256, D), f32)
    xp[256:256 + S] = x
    rpb = np.asarray(inputs["na_rpb"][0], f32)
    ws = _wshards(inputs)
    maps = []
    for cidx in range(NCORES):
        start = cidx * T
        m = dict(com)
        m["x_ext"] = np.ascontiguousarray(xp[start:start + 1536])
        m["na_bias"] = _na_bias(rpb, cidx)
        m["sw_mask"] = _sw_mask(cidx)
        m["sinks"] = np.ascontiguousarray(np.broadcast_to(np.asarray(inputs["sw_sinks"][0], f32)[None, :], (128, 8)))
        m["rope"] = _rope_table(start)
        m["wshard"] = ws[cidx]
        maps.append(m)
    return maps


def _maps_B_extra(inputs, m):
    f32 = np.float32
    lv = np.stack([np.asarray(inputs[k][0], f32) for k in ("diff_lam_q1", "diff_lam_k1", "diff_lam_q2", "diff_lam_k2")])
    m["lamvec"] = np.ascontiguousarray(np.broadcast_to(lv[None], (128, 4, 128)))
    m["subln"] = np.ascontiguousarray(np.broadcast_to(np.asarray(inputs["diff_subln_g"][0], f32)[None], (128, 256)))


MODE = "fused"


def kernel(**inputs):
    com = _host_consts(inputs)
    cores = list(range(NCORES))
    if MODE == "fused":
        maps = _maps_A(inputs, com)
        for m in maps:
            _maps_B_extra(inputs, m)
        res = run_bass_kernel_spmd(_get("fused"), maps, core_ids=cores)
        outs = [r["out"] for r in res.results]
    else:
        mapsA = _maps_A(inputs, com)
        resA = run_bass_kernel_spmd(_get("A"), mapsA, core_ids=cores).results
        kfull = np.concatenate([r["kloc"] for r in resA], axis=0)
        vfull = np.concatenate([r["vloc"] for r in resA], axis=0)
        mapsB = []
        for cidx in range(NCORES):
            m = dict(com)
            m["wshard"] = mapsA[cidx]["wshard"]
            m["rope"] = mapsA[cidx]["rope"]
            _maps_B_extra(inputs, m)
            m["hT_in"] = resA[cidx]["hT"]
            m["qT1_d"] = resA[cidx]["qT1_d"]
            m["kfull"] = kfull
            m["vfull"] = vfull
            mapsB.append(m)
        resB = run_bass_kernel_spmd(_get("B"), mapsB, core_ids=cores).results
        outs = [r["out"] for r in resB]
    return np.concatenate(outs, axis=0).reshape(1, S, D).astype(np.float32)
```
